# Optimizing a Trainium2 kernel written in Bass

```python
import math
import jax, jax.numpy as jnp
from jax import lax
import numpy as np

D_MODEL = 1024
BATCH = 8
SEQ = 4096
DEPTH = 2

D_FF = 2816
HEAD_DIM = 64
W_A = 256
A_BLOCKS = 4
A_BLOCK_W = W_A // A_BLOCKS
LRU_C = 8.0
LRU_CONV = 4
N_Q_HEADS = 8
N_KV_HEADS = 2
W_B = N_Q_HEADS * HEAD_DIM
WINDOW = 128
BLK = 128
W_C = 256
C_GROUPS = 4
C_CONV = 31
D_MIX = W_A + W_B + W_C
OFF_LRU_X = 0
OFF_LRU_GATE = OFF_LRU_X + W_A
OFF_Q = OFF_LRU_GATE + W_A
OFF_K = OFF_Q + W_B
OFF_V = OFF_K + N_KV_HEADS * HEAD_DIM
OFF_GLU = OFF_V + N_KV_HEADS * HEAD_DIM
D_IN_PROJ = OFF_GLU + 2 * W_C
NORM_EPS = 1e-6
LN_EPS = 1e-5
NEG_BIG = -1e30

kernel_name = "hymba_style_lru_swa_conformer_macaron"


def rms_norm(x, g):
    xf = x.astype(jnp.float32)
    y = xf * lax.rsqrt(jnp.mean(xf * xf, axis=-1, keepdims=True) + NORM_EPS)
    return (y * g.astype(jnp.float32)).astype(x.dtype)


def layer_norm(x, g, b):
    xf = x.astype(jnp.float32)
    mu = jnp.mean(xf, axis=-1, keepdims=True)
    xc = xf - mu
    var = jnp.mean(xc * xc, axis=-1, keepdims=True)
    y = xc * lax.rsqrt(var + LN_EPS) * g.astype(jnp.float32) + b.astype(jnp.float32)
    return y.astype(x.dtype)


def swiglu(x, w_gu, w_down):
    g, u = jnp.split(x @ w_gu, 2, axis=-1)
    return (jax.nn.silu(g) * u) @ w_down


def causal_depthwise_conv(x, w, b):
    k = w.shape[0]
    y = lax.conv_general_dilated(
        x, w[:, None, :], window_strides=(1,), padding=[(k - 1, 0)],
        dimension_numbers=("NWC", "WIO", "NWC"), feature_group_count=x.shape[-1])
    return y + b


def rg_lru(x, w_a, b_a, w_x, b_x, lam):
    bsz, s, w = x.shape
    xb = x.reshape(bsz, s, A_BLOCKS, A_BLOCK_W)
    r = jax.nn.sigmoid(jnp.einsum("bshi,hij->bshj", xb, w_a).reshape(bsz, s, w) + b_a)
    i = jax.nn.sigmoid(jnp.einsum("bshi,hij->bshj", xb, w_x).reshape(bsz, s, w) + b_x)
    log_a = -LRU_C * r.astype(jnp.float32) * jax.nn.softplus(-lam.astype(jnp.float32))
    a = jnp.exp(log_a)
    u = jnp.sqrt(-jnp.expm1(2.0 * log_a)) * (i * x).astype(jnp.float32)

    def combine(left, right):
        a1, b1 = left
        a2, b2 = right
        return a1 * a2, a2 * b1 + b2

    _, h = lax.associative_scan(combine, (a, u), axis=1)
    return h.astype(x.dtype)


def sliding_window_attention_sinks(q, k, v, sinks):
    bsz, s, h, d = q.shape
    kvh = k.shape[2]
    grp = h // kvh
    nblk = s // BLK
    qb = q.reshape(bsz, nblk, BLK, kvh, grp, d)

    def banded(t):
        cur = t.reshape(bsz, nblk, BLK, kvh, d)
        prev = jnp.pad(t, ((0, 0), (BLK, 0), (0, 0), (0, 0)))[:, :s].reshape(bsz, nblk, BLK, kvh, d)
        return jnp.concatenate([prev, cur], axis=2)

    kw, vw = banded(k), banded(v)
    scores = jnp.einsum("bnqkgd,bnjkd->bnkgqj", qb, kw).astype(jnp.float32) * (1.0 / math.sqrt(d))
    qi = jnp.arange(BLK)[:, None]
    kj = jnp.arange(2 * BLK)[None, :]
    rel = BLK + qi - kj
    k_pos = (jnp.arange(nblk)[:, None, None] - 1) * BLK + kj[None]
    mask = (rel >= 0)[None] & (rel < WINDOW)[None] & (k_pos >= 0)
    scores = jnp.where(mask[None, :, None, None], scores, NEG_BIG)
    sink = sinks.astype(jnp.float32).reshape(1, 1, kvh, grp, 1, 1)
    m = jnp.maximum(jnp.max(scores, axis=-1, keepdims=True), sink)
    p = jnp.exp(scores - m)
    p = p / (jnp.sum(p, axis=-1, keepdims=True) + jnp.exp(sink - m))
    o = jnp.einsum("bnkgqj,bnjkd->bnqkgd", p.astype(v.dtype), vw)
    return o.reshape(bsz, s, h * d)


def conformer_conv(glu_in, w, b, ln_g, ln_b):
    a, g = jnp.split(glu_in, 2, axis=-1)
    y = a * jax.nn.sigmoid(g)
    y = causal_depthwise_conv(y, w, b)
    y = layer_norm(y, ln_g, ln_b)
    return jax.nn.silu(y)


def setup_inputs(seed: int = 0) -> dict:
    key = jax.random.key(seed)
    ks = iter(jax.random.split(key, 40))
    L = DEPTH

    def nrm(shape, scale):
        return scale * jax.random.normal(next(ks), shape, jnp.float32)

    def gain(shape):
        return 1.0 + 0.05 * jax.random.normal(next(ks), shape, jnp.float32)

    a0 = jax.random.uniform(next(ks), (L, W_A), jnp.float32, minval=0.9, maxval=0.999)
    return {
        "x": jax.random.normal(next(ks), (BATCH, SEQ, D_MODEL), jnp.float32),
        "ffn1_pre_g": gain((L, D_MODEL)),
        "ffn1_w_gu": nrm((L, D_MODEL, 2 * D_FF), D_MODEL ** -0.5),
        "ffn1_w_down": nrm((L, D_FF, D_MODEL), D_FF ** -0.5),
        "ffn1_post_g": gain((L, D_MODEL)),
        "mix_pre_g": gain((L, D_MODEL)),
        "w_in": nrm((L, D_MODEL, D_IN_PROJ), D_MODEL ** -0.5),
        "lru_conv_w": nrm((L, LRU_CONV, W_A), LRU_CONV ** -0.5),
        "lru_conv_b": nrm((L, W_A), 0.02),
        "lru_w_a": nrm((L, A_BLOCKS, A_BLOCK_W, A_BLOCK_W), A_BLOCK_W ** -0.5),
        "lru_b_a": nrm((L, W_A), 0.02),
        "lru_w_x": nrm((L, A_BLOCKS, A_BLOCK_W, A_BLOCK_W), A_BLOCK_W ** -0.5),
        "lru_b_x": nrm((L, W_A), 0.02),
        "lru_lambda": jnp.log(a0) - jnp.log1p(-a0),
        "attn_sinks": nrm((L, N_Q_HEADS), 0.5),
        "conv_w": nrm((L, C_CONV, W_C), C_CONV ** -0.5),
        "conv_b": nrm((L, W_C), 0.02),
        "conv_ln_g": gain((L, W_C)),
        "conv_ln_b": nrm((L, W_C), 0.02),
        "group_g": gain((L, D_MIX)),
        "w_out": nrm((L, D_MIX, D_MODEL), D_MIX ** -0.5),
        "mix_post_g": gain((L, D_MODEL)),
        "ffn2_pre_g": gain((L, D_MODEL)),
        "ffn2_w_gu": nrm((L, D_MODEL, 2 * D_FF), D_MODEL ** -0.5),
        "ffn2_w_down": nrm((L, D_FF, D_MODEL), D_FF ** -0.5),
        "ffn2_post_g": gain((L, D_MODEL)),
    }


def reference(x, ffn1_pre_g, ffn1_w_gu, ffn1_w_down, ffn1_post_g, mix_pre_g, w_in,
              lru_conv_w, lru_conv_b, lru_w_a, lru_b_a, lru_w_x, lru_b_x, lru_lambda,
              attn_sinks, conv_w, conv_b, conv_ln_g, conv_ln_b, group_g, w_out,
              mix_post_g, ffn2_pre_g, ffn2_w_gu, ffn2_w_down, ffn2_post_g):
    bsz, s, _ = x.shape
    for l in range(DEPTH):
        x = x + 0.5 * rms_norm(swiglu(rms_norm(x, ffn1_pre_g[l]), ffn1_w_gu[l], ffn1_w_down[l]), ffn1_post_g[l])

        hn = rms_norm(x, mix_pre_g[l])
        proj = hn @ w_in[l]
        lru_x = proj[..., OFF_LRU_X:OFF_LRU_GATE]
        lru_gate = proj[..., OFF_LRU_GATE:OFF_Q]
        q = proj[..., OFF_Q:OFF_K].reshape(bsz, s, N_Q_HEADS, HEAD_DIM)
        k = proj[..., OFF_K:OFF_V].reshape(bsz, s, N_KV_HEADS, HEAD_DIM)
        v = proj[..., OFF_V:OFF_GLU].reshape(bsz, s, N_KV_HEADS, HEAD_DIM)
        glu_in = proj[..., OFF_GLU:]

        y_a = jax.nn.gelu(lru_gate) * rg_lru(
            causal_depthwise_conv(lru_x, lru_conv_w[l], lru_conv_b[l]),
            lru_w_a[l], lru_b_a[l], lru_w_x[l], lru_b_x[l], lru_lambda[l])
        y_b = sliding_window_attention_sinks(q, k, v, attn_sinks[l])
        y_c = conformer_conv(glu_in, conv_w[l], conv_b[l], conv_ln_g[l], conv_ln_b[l])

        gg = group_g[l]
        y = jnp.concatenate([
            rms_norm(y_a, gg[:W_A]),
            rms_norm(y_b, gg[W_A:W_A + W_B]),
            rms_norm(y_c, gg[W_A + W_B:]),
        ], axis=-1)
        x = x + rms_norm(y @ w_out[l], mix_post_g[l])

        x = x + 0.5 * rms_norm(swiglu(rms_norm(x, ffn2_pre_g[l]), ffn2_w_gu[l], ffn2_w_down[l]), ffn2_post_g[l])
    return x
```

```python
import math
from contextlib import ExitStack

import numpy as np
import concourse.bass as bass
import concourse.mybir as mybir
from concourse.bass_utils import run_bass_kernel_spmd

F32 = mybir.dt.float32
BF16 = mybir.dt.bfloat16
AF = mybir.ActivationFunctionType
ALU = mybir.AluOpType

D = 1024
SEQ = 4096
DFF = 2816
NJ = 22
DIN = 1792
TCH = 512
NT = TCH // 128
NCH = SEQ // TCH
NSLOT = 5
SLAB = 4096
EPS = 1e-6
LN_EPS = 1e-5

ENGS = ("sp", "act", "dve", "pool", "pe")


class Inst:
    __slots__ = ("eng", "emit", "dma", "seq", "signal", "waits", "snap", "id",
                 "semi", "semv", "prewait")

    def __init__(self, eng, emit, dma):
        self.eng = eng
        self.emit = emit
        self.dma = dma
        self.signal = False
        self.waits = []
        self.prewait = None


def _merge(iv):
    iv.sort()
    out = [list(iv[0])]
    for a, b in iv[1:]:
        if a <= out[-1][1]:
            if b > out[-1][1]:
                out[-1][1] = b
        else:
            out.append([a, b])
    return out


def region(ap):
    t = ap.tensor
    name = t.name
    esz = mybir.dt.size(ap.dtype)
    dims = [list(d) for d in ap.ap]
    space = str(type(t).__name__)
    if "DRam" in space:
        return (name, 0, 1, [[0, 1 << 60]], 0, 1 << 60, True)
    pstep, pcnt = dims[0]
    off = ap.offset
    if pstep == 0:
        p0, col = 0, off
        pcnt = 1
    else:
        p0, col = off // pstep, off % pstep
    free = [d for d in dims[1:] if d[1] > 1 and d[0] != 0]
    free.sort(key=lambda d: -abs(d[0]))
    if not free:
        iv = [[col, col + 1]]
    else:
        inner = free[-1]
        outer = free[:-1]
        n_outer = 1
        for d in outer:
            n_outer *= d[1]
        if inner[0] != 1 or n_outer > 256:
            lo = col + sum(min(0, d[0] * (d[1] - 1)) for d in free)
            hi = col + sum(max(0, d[0] * (d[1] - 1)) for d in free) + 1
            iv = [[lo, hi]]
        else:
            starts = [col]
            for d in outer:
                starts = [s + d[0] * i for s in starts for i in range(d[1])]
            iv = _merge([[s, s + inner[1]] for s in starts])
    iv = [[a * esz, b * esz] for a, b in iv]
    return (name, p0, p0 + pcnt, iv, iv[0][0], iv[-1][1], False)


def _overlap(r, q):
    if r[2] <= q[1] or q[2] <= r[1] or r[5] <= q[4] or q[5] <= r[4]:
        return False
    a, b = r[3], q[3]
    i = j = 0
    while i < len(a) and j < len(b):
        if a[i][1] <= b[j][0]:
            i += 1
        elif b[j][1] <= a[i][0]:
            j += 1
        else:
            return True
    return False


def _contains(r, q):
    if q[1] < r[1] or q[2] > r[2] or q[4] < r[4] or q[5] > r[5]:
        return False
    a = r[3]
    for s, e in q[3]:
        ok = False
        for s2, e2 in a:
            if s2 <= s and e <= e2:
                ok = True
                break
        if not ok:
            return False
    return True


class Prog:
    NDSEM = 8
    EPOCH = 30000

    def __init__(self, nc):
        self.nc = nc
        self.lists = {e: [] for e in ENGS}
        self.recs = {}
        self.known = {e: {f: -1 for f in ENGS} for e in ENGS}
        self.known_dma = {e: set() for e in ENGS}
        self.dma_hist = {e: [] for e in ENGS}
        self.nid = 0
        self.dry = False

    def add(self, eng, emit, reads, writes, dma=False):
        if self.dry:
            return None
        inst = Inst(eng, emit, dma)
        inst.id = self.nid
        self.nid += 1
        inst.seq = len(self.lists[eng])
        deps = {}
        rregs = [region(a) for a in reads]
        wregs = [r_ for r_ in (region(a) for a in writes) if r_[0] != "junk"]
        for r in rregs:
            for rec in self.recs.get(r[0], ()):
                if rec[1] and _overlap(r, rec[0]):
                    deps[rec[2].id] = (rec[2], True)
        for w in wregs:
            for rec in self.recs.get(w[0], ()):
                if w[6] and rec[1]:
                    continue
                if _overlap(w, rec[0]):
                    if rec[2].id not in deps:
                        deps[rec[2].id] = (rec[2], False)
        kn = self.known[eng]
        for jid in sorted(deps):
            J, raw = deps[jid]
            if J.dma:
                if J.id in self.known_dma[eng]:
                    continue
                J.signal = True
                inst.waits.append(J)
                self.known_dma[eng].add(J.id)
                for f, v in J.snap.items():
                    if v > kn[f]:
                        kn[f] = v
            else:
                F = J.eng
                if F == eng and F == "pe":
                    continue
                if kn[F] >= J.seq:
                    continue
                J.signal = True
                inst.waits.append(J)
                kn[F] = J.seq
                for f, v in J.snap.items():
                    if v > kn[f]:
                        kn[f] = v
        if dma:
            h = self.dma_hist[eng]
            if len(h) >= self.NDSEM:
                prev = h[len(h) - self.NDSEM]
                prev.signal = True
                inst.prewait = prev
                self.known_dma[eng].add(prev.id)
            h.append(inst)
        inst.snap = dict(kn)
        for r in rregs:
            lst = self.recs.setdefault(r[0], [])
            done = False
            if not dma:
                for rec in lst:
                    if (not rec[1]) and rec[2].eng == eng and not rec[2].dma \
                            and rec[0][1:4] == r[1:4]:
                        rec[2] = inst
                        done = True
                        break
            if not done:
                lst.append([r, False, inst])
        for w in wregs:
            lst = self.recs.setdefault(w[0], [])
            if w[6]:
                lst.append([w, True, inst])
            else:
                lst[:] = [rec for rec in lst if not _contains(w, rec[0])]
                lst.append([w, True, inst])
        self.lists[eng].append(inst)
        return inst

    def emit_all(self, es):
        nc = self.nc
        sems = {}
        for e in ENGS:
            n = 0
            for inst in self.lists[e]:
                if inst.dma or not inst.signal:
                    continue
                inst.semi = (e, n // self.EPOCH)
                inst.semv = n % self.EPOCH + 1
                n += 1
            nep = n // self.EPOCH + 1
            for k in range(nep):
                sems[(e, k)] = es.enter_context(nc.semaphore(f"c_{e}_{k}"))
            h = self.dma_hist[e]
            for i, inst in enumerate(h):
                inst.semi = (e, "d", i % self.NDSEM)
                inst.semv = 16 * (i // self.NDSEM + 1)
            if h:
                for k in range(self.NDSEM):
                    sems[(e, "d", k)] = es.enter_context(nc.semaphore(f"d_{e}_{k}"))
        lists = self.lists

        def run(engname, eng):
            for inst in lists[engname]:
                if inst.prewait is not None:
                    p = inst.prewait
                    eng.wait_ge(sems[p.semi], p.semv)
                for J in inst.waits:
                    eng.wait_ge(sems[J.semi], J.semv)
                bi = inst.emit(eng)
                if inst.dma:
                    bi.then_inc(sems[inst.semi], 16)
                elif inst.signal:
                    bi.then_inc(sems[inst.semi], 1)

        with nc.Block() as block:
            @block.sync
            def _(e):
                run("sp", e)

            @block.scalar
            def _(e):
                run("act", e)

            @block.vector
            def _(e):
                run("dve", e)

            @block.gpsimd
            def _(e):
                run("pool", e)

            @block.tensor
            def _(e):
                run("pe", e)

    def act(self, out, in_, func, bias=None, scale=None, accum=None):
        reads = [in_]
        kw = {}
        if bias is not None:
            kw["bias"] = bias
            if not isinstance(bias, (int, float)):
                reads.append(bias)
        if scale is not None:
            kw["scale"] = scale
            if not isinstance(scale, (int, float)):
                reads.append(scale)
        writes = [out]
        if accum is not None:
            kw["accum_out"] = accum
            writes.append(accum)
        return self.add("act", lambda e: e.activation(out, in_, func, **kw), reads, writes)

    def tt(self, eng, out, a, b, op):
        return self.add(eng, lambda e: e.tensor_tensor(out, a, b, op), [a, b], [out])

    def ts(self, eng, out, a, s1, op0, s2=None, op1=None):
        reads = [a]
        if not isinstance(s1, (int, float)):
            reads.append(s1)
        if s2 is not None and not isinstance(s2, (int, float)):
            reads.append(s2)
        if op1 is None:
            return self.add(eng, lambda e: e.tensor_scalar(out, a, s1, None, op0), reads, [out])
        return self.add(eng, lambda e: e.tensor_scalar(out, a, s1, s2, op0, op1), reads, [out])

    def stt(self, out, in0, scalar, in1, op0, op1):
        reads = [in0, in1]
        if not isinstance(scalar, (int, float)):
            reads.append(scalar)
        return self.add("dve", lambda e: e.scalar_tensor_tensor(out, in0, scalar, in1, op0, op1),
                        reads, [out])

    def copy(self, eng, out, in_):
        if eng == "act":
            return self.add("act", lambda e: e.activation(out, in_, AF.Copy), [in_], [out])
        return self.add(eng, lambda e: e.tensor_copy(out, in_), [in_], [out])

    def recip(self, out, in_):
        return self.add("dve", lambda e: e.reciprocal(out, in_), [in_], [out])

    def memset(self, eng, out, val):
        return self.add(eng, lambda e: e.memset(out, val), [], [out])

    def mm(self, out, lhsT, rhs, start, stop):
        return self.add("pe", lambda e: e.matmul(out, lhsT, rhs, start=start, stop=stop),
                        [lhsT, rhs], [out])

    def tr(self, out, in_, ident):
        return self.add("pe", lambda e: e.transpose(out, in_, ident), [in_, ident], [out])

    def dma(self, q, out, in_, **kw):
        return self.add(q, lambda e: e.dma_start(out, in_, **kw), [in_], [out], dma=True)


WNAMES = ["ffn1_pre_g", "ffn1_w_gu", "ffn1_w_down", "ffn1_post_g", "mix_pre_g", "w_in",
          "lru_conv_w", "lru_conv_b", "lru_w_a", "lru_b_a", "lru_w_x", "lru_b_x",
          "lru_lambda", "attn_sinks", "conv_w", "conv_b", "conv_ln_g", "conv_ln_b",
          "group_g", "w_out", "mix_post_g", "ffn2_pre_g", "ffn2_w_gu", "ffn2_w_down",
          "ffn2_post_g"]
WSHAPES = {
    "ffn1_pre_g": (2, 1024), "ffn1_w_gu": (2, 1024, 5632), "ffn1_w_down": (2, 2816, 1024),
    "ffn1_post_g": (2, 1024), "mix_pre_g": (2, 1024), "w_in": (2, 1024, 1792),
    "lru_conv_w": (2, 4, 256), "lru_conv_b": (2, 256), "lru_w_a": (2, 4, 64, 64),
    "lru_b_a": (2, 256), "lru_w_x": (2, 4, 64, 64), "lru_b_x": (2, 256),
    "lru_lambda": (2, 256), "attn_sinks": (2, 8), "conv_w": (2, 31, 256), "conv_b": (2, 256),
    "conv_ln_g": (2, 256), "conv_ln_b": (2, 256), "group_g": (2, 1024), "w_out": (2, 1024, 1024),
    "mix_post_g": (2, 1024), "ffn2_pre_g": (2, 1024), "ffn2_w_gu": (2, 1024, 5632),
    "ffn2_w_down": (2, 2816, 1024), "ffn2_post_g": (2, 1024),
}

PC = {}
_o = 0
for _n, _w in [("g1", 8), ("gm", 8), ("g2", 8), ("gg", 8), ("lcw", 8), ("lcb", 2), ("ba", 2),
               ("bx", 2), ("lam", 2), ("c1", 2), ("c2", 2), ("nba", 2), ("nbx", 2), ("ccw", 62), ("ccb", 2),
               ("lng", 2), ("lnb", 2), ("snk", 8), ("esnk", 8)]:
    PC[_n] = (_o, _w)
    _o += _w
NPAR = _o


def build(depth=2, stages=("f1", "mix", "f2"), nch=NCH, seq=SEQ):
    nc = bass.Bass("TRN2", target_bir_lowering=False, dynamic_dma_scratch_size=512)
    P = Prog(nc)
    dr = {}
    x_d = nc.dram_tensor("x", [seq, D], F32, kind="ExternalInput").ap()
    for n in WNAMES:
        dr[n] = nc.dram_tensor(n, list(WSHAPES[n]), F32, kind="ExternalInput").ap()
    out_d = nc.dram_tensor("out", [seq, D], F32, kind="ExternalOutput").ap()
    scr = {}
    for l in range(depth):
        for nm, ns in (("gu1", 11), ("dn1", 6), ("win", 4), ("wout", 2), ("gu2", 11), ("dn2", 6)):
            scr[(nm, l)] = nc.dram_tensor(f"scr_{nm}_{l}", [ns, 128, SLAB], BF16, kind="Internal").ap()

    with ExitStack() as es:
        def sb(name, shape, dt):
            return es.enter_context(nc.sbuf_tensor(name, shape, dt))

        def pst(name, shape, dt):
            return es.enter_context(nc.psum_tensor(name, shape, dt))

        XR = [sb(f"xres{i}", [128, NT, D], F32) for i in range(2)]
        XNTS = [sb(f"xnT{i}", [128, 8, TCH], BF16) for i in range(2)]
        MS = sb("ms", [128, 9728], BF16)
        U = sb("U", [128, NSLOT * SLAB + NJ * TCH], BF16)
        GP = sb("gpost", [128, 2, D], F32)
        PAR = sb("par", [128, 2, NPAR], F32)
        IDB = sb("identb", [128, 128], BF16)
        ONES = sb("ones32", [128, 128], F32)
        XNB = sb("xnb", [128, 1, D], BF16)
        JUNK = sb("junk", [128, D], BF16)
        SM = sb("small", [128, 128], F32)
        SGF = sb("sgf", [128, 2, TCH], F32)
        MF = sb("mf", [128, 24, 512], F32)
        IDF = MF[:, 0, 0:128]
        YB0 = MF[:, 7:11, :]
        DG = sb("diag", [128, 8, 128], BF16)
        KT = [sb(f"kT{l}", [128, 128 + TCH], BF16) for l in range(depth)]
        VA = [sb(f"va{l}", [128, NT + 1, 2, 65], BF16) for l in range(depth)]
        LX = [sb(f"lx{l}", [128, 2, 3 + TCH], BF16) for l in range(depth)]
        CIN = [sb(f"cin{l}", [128, 2, 30 + TCH], BF16) for l in range(depth)]
        HST = [sb(f"hst{l}", [128, 2], F32) for l in range(depth)]
        WA = [sb(f"wa{l}", [128, 2, 128], BF16) for l in range(depth)]
        WX = [sb(f"wx{l}", [128, 2, 128], BF16) for l in range(depth)]
        TMPE = sb("tmpe", [128, 2, 512], F32)
        MSK = sb("msk", [128, 2, 512], BF16)
        Q = [pst(f"ps{i}", [128, 512], F32) for i in range(7)]
        PT = pst("pst", [128, 1024], BF16)

        slots = [U[:, i * SLAB:(i + 1) * SLAB].rearrange("p (a b) -> p a b", a=8) for i in range(NSLOT)]
        HT = U[:, NSLOT * SLAB:NSLOT * SLAB + NJ * TCH].rearrange("p (a b) -> p a b", a=NJ)
        QT = MS[:, 0:2048].rearrange("p (a b) -> p a b", a=4)
        PTB = MS[:, 2048:4096].rearrange("p (g a b) -> p g a b", g=2, a=2)
        YT = MS[:, 4096:8192].rearrange("p (a b) -> p a b", a=8)
        YBN = MS[:, 8192:8704]
        XCB = MS[:, 8704:9728].rearrange("p (a b) -> p a b", a=2)
        GATE = MF[:, 0:2, :]
        YC = MF[:, 0:2, :]
        YA = MF[:, 2:4, :]
        ZZ = MF[:, 2:4, :]
        SQ = MF[:, 4:6, :]
        YB = MF[:, 6, :]

        def T(i):
            return MF[:, 7 + i, :]

        def T2(i):
            return MF[:, 7 + 2 * i:9 + 2 * i, :]

        ST32 = [U[:, i * 11264:(i + 1) * 11264].bitcast(F32) for i in range(2)]
        ST16 = [U[:, 22528:28160], MS[:, 0:5632]]

        smc = [0, 0]
        cur_k = [0]

        def small(n=1):
            k = cur_k[0]
            c = smc[k]
            if c + n > 64:
                c = 0
            smc[k] = c + n
            return SM[:, k * 64 + c:k * 64 + c + n]

        def par(l, name, i=0, n=1):
            o, w = PC[name]
            return PAR[:, l, o + i:o + i + n]

        P.memset("pool", IDF[:], 0.0)
        P.add("pool", lambda e: e.affine_select(IDF[:], IDF[:], [[1, 128]], ALU.not_equal, 1.0,
                                                base=0, channel_multiplier=-1), [IDF[:]], [IDF[:]])
        P.copy("dve", IDB[:], IDF[:])
        P.memset("pool", MSK[:], 0.0)
        P.add("pool", lambda e: e.affine_select(MSK[:, 0, :], MSK[:, 0, :], [[0, 4], [1, 128]], ALU.is_ge, -30000.0,
                                                base=0, channel_multiplier=-1), [MSK[:, 0, :]], [MSK[:, 0, :]])
        P.add("pool", lambda e: e.affine_select(MSK[:, 1, :], MSK[:, 1, :], [[0, 4], [-1, 128]], ALU.is_gt, -30000.0,
                                                base=0, channel_multiplier=1), [MSK[:, 1, :]], [MSK[:, 1, :]])
        P.memset("dve", ONES[:], 1.0 / 256.0)

        def load_cols(dst, vec, ncols):
            src = bass.AP(vec.tensor, vec.offset, [[1, 128], [128, ncols]])
            P.dma("sp", dst, src, allow_slow_non_contiguous=True)

        for l in range(depth):
            for nm, key in (("g1", "ffn1_pre_g"), ("gm", "mix_pre_g"), ("g2", "ffn2_pre_g"),
                            ("gg", "group_g")):
                load_cols(par(l, nm, 0, 8), dr[key][l], 8)
            for nm, key in (("lcb", "lru_conv_b"), ("ba", "lru_b_a"), ("bx", "lru_b_x"),
                            ("lam", "lru_lambda"), ("ccb", "conv_b"), ("lng", "conv_ln_g"),
                            ("lnb", "conv_ln_b")):
                load_cols(par(l, nm, 0, 2), dr[key][l], 2)
            for nm, key, K in (("lcw", "lru_conv_w", 4), ("ccw", "conv_w", 31)):
                v = dr[key][l]
                for cc in range(2):
                    src = bass.AP(v.tensor, v.offset + cc * 128, [[1, 128], [256, K]])
                    P.dma("sp", par(l, nm, cc * K, K), src, allow_slow_non_contiguous=True)
            v = dr["attn_sinks"][l]
            P.dma("sp", par(l, "snk", 0, 8), bass.AP(v.tensor, v.offset, [[0, 128], [1, 8]]))
            t1 = small(2)
            P.act(t1, par(l, "lam", 0, 2), AF.Exp, scale=-1.0)
            t2 = small(2)
            P.act(t2, t1, AF.Ln, bias=1.0)
            P.ts("dve", par(l, "c1", 0, 2), t2, -8.0, ALU.mult)
            P.ts("dve", par(l, "c2", 0, 2), t2, -16.0, ALU.mult)
            P.act(par(l, "esnk", 0, 8), par(l, "snk", 0, 8), AF.Exp)
            P.ts("dve", par(l, "nba", 0, 2), par(l, "ba", 0, 2), -1.0, ALU.mult)
            P.ts("dve", par(l, "nbx", 0, 2), par(l, "bx", 0, 2), -1.0, ALU.mult)

        if "mix" in stages:
            for l in range(depth):
                P.memset("pool", LX[l][:, :, 0:3], 0.0)
                P.memset("pool", CIN[l][:, :, 0:30], 0.0)
                P.memset("pool", HST[l][:], 0.0)
                P.memset("pool", VA[l][:], 1.0)
                for wt, key in ((WA, "lru_w_a"), (WX, "lru_w_x")):
                    stg = MF[:, 0:1, 0:256].rearrange("p a (c q) -> p (a c) q", c=2)
                    P.memset("dve", stg, 0.0)
                    for cc in range(2):
                        for hh in range(2):
                            P.dma("sp", stg[hh * 64:(hh + 1) * 64, cc, hh * 64:(hh + 1) * 64],
                                  dr[key][l][2 * cc + hh])
                    P.copy("dve", wt[l][:], stg)

        cast_rr = [0]

        def cast(out, in_, scal):
            e = ("dve", "act")[cast_rr[0] % 2]
            cast_rr[0] += 1
            if e == "act":
                if scal is None:
                    P.copy("act", out, in_)
                else:
                    P.act(out, in_, AF.Copy, scale=scal)
            else:
                if scal is None:
                    P.copy(e, out, in_)
                else:
                    P.ts(e, out, in_, scal, ALU.mult)

        stc = [0]

        def stage_bufs():
            i = stc[0] % 2
            stc[0] += 1
            return ST32[i], ST16[i]

        def conv_gu(l, key, gname, sname):
            w = dr[key][l]
            dst = scr[(sname, l)].rearrange("s p (k c) -> p s k c", k=8)
            for kc in range(8):
                s32, s16 = stage_bufs()
                P.dma("sp", s32[:, :], w[kc * 128:(kc + 1) * 128, :])
                o = s16[:, :].rearrange("p (s h c) -> p s h c", s=11, h=2)
                i_ = s32[:, :].rearrange("p (h s c) -> p s h c", h=2, s=11)
                for h in range(2):
                    cast(o[:, :, h, :], i_[:, :, h, :], par(l, gname, kc))
                P.dma("sp", dst[:, :, kc, :], s16[:, :].rearrange("p (s c) -> p s c", s=11))

        def conv_dn(l, key, sname):
            w = dr[key][l].rearrange("(j p) c -> p j c", p=128)
            dst = scr[(sname, l)].rearrange("s p (j c) -> p s j c", j=8)
            groups = [(0, 0, 4), (0, 4, 4), (1, 0, 4), (1, 4, 4), (2, 0, 3), (2, 3, 3)]
            for g, j0, nj in groups:
                jg = g * 8 + j0
                s32, s16 = stage_bufs()
                P.dma("sp", s32[:, 0:nj * 1024].rearrange("p (j c) -> p j c", j=nj), w[:, jg:jg + nj, :])
                cast(s16[:, 0:nj * 1024], s32[:, 0:nj * 1024], None)
                v16 = s16[:, 0:nj * 1024].rearrange("p (j c) -> p j c", j=nj)
                for hh in range(2):
                    P.dma("sp", dst[:, hh * 3 + g, j0:j0 + nj, :], v16[:, :, hh * 512:(hh + 1) * 512])

        def conv_win(l):
            w = dr["w_in"][l]
            dst = scr[("win", l)].rearrange("s p (k c) -> p s k c", k=8)
            for kc in range(8):
                s32, s16 = stage_bufs()
                P.dma("sp", s32[:, 0:DIN], w[kc * 128:(kc + 1) * 128, :])
                g = par(l, "gm", kc)
                cast(s16[:, 0:512], s32[:, 0:512], g)
                o = s16[:, 512:1024].rearrange("p (c h d) -> p c h d", c=4, h=2)
                i_ = s32[:, 512:1024].rearrange("p (h c d) -> p c h d", h=2, c=4)
                for h in range(2):
                    cast(o[:, :, h, :], i_[:, :, h, :], g)
                cast(s16[:, 1024:DIN], s32[:, 1024:DIN], g)
                P.memset("pool", s16[:, DIN:2048], 0.0)
                P.dma("sp", dst[:, :, kc, :], s16[:, 0:2048].rearrange("p (s c) -> p s c", s=4))

        def conv_wout(l):
            w = dr["w_out"][l]
            dst = scr[("wout", l)].rearrange("s p (k c) -> p s k c", k=8)
            for dc in range(8):
                s32, s16 = stage_bufs()
                P.dma("sp", s32[:, 0:1024], w[dc * 128:(dc + 1) * 128, :])
                cast(s16[:, 0:1024], s32[:, 0:1024], par(l, "gg", dc))
                P.dma("sp", dst[:, :, dc, :], s16[:, 0:1024].rearrange("p (s c) -> p s c", s=2))

        for l in range(depth):
            if "f1" in stages:
                conv_gu(l, "ffn1_w_gu", "g1", "gu1")
                conv_dn(l, "ffn1_w_down", "dn1")
            if "mix" in stages:
                conv_win(l)
                conv_wout(l)
            if "f2" in stages:
                conv_gu(l, "ffn2_w_gu", "g2", "gu2")
                conv_dn(l, "ffn2_w_down", "dn2")

        ring = {"plan": [], "issued": 0, "next": 0, "out": set()}

        def ring_reset():
            ring["issued"] = 0
            ring["next"] = 0
            ring["out"] = set()
            smc[0] = smc[1] = 0

        def issue_to(n):
            plan = ring["plan"]
            while ring["issued"] < min(n, len(plan)):
                i = ring["issued"]
                key, si_ = plan[i]
                nval = 6 * 512 if (key[0].startswith("dn") and si_ % 3 == 2) else SLAB
                P.dma("sp", U[:, (i % NSLOT) * SLAB:(i % NSLOT) * SLAB + nval], scr[key][si_][:, 0:nval])
                ring["issued"] += 1

        def next_slab(key, si_):
            i = ring["next"]
            ring["next"] += 1
            if P.dry:
                ring["plan"].append((key, si_))
                return slots[0], i
            assert ring["plan"][i] == (key, si_), (i, ring["plan"][i], key, si_)
            lo = min(ring["out"]) if ring["out"] else i
            lim = min(i + NSLOT - 1, lo + NSLOT)
            assert lim >= i + 1, (i, lo)
            issue_to(lim)
            ring["out"].add(i)
            return slots[i % NSLOT], i

        def release(i):
            ring["out"].discard(i)

        def load_x(c):
            buf = XR[c % 2]
            src = x_d[c * TCH:(c + 1) * TCH, :].rearrange("(t p) d -> p t d", p=128)
            P.dma("sp", buf[:], src)

        def store_x(c):
            buf = XR[c % 2]
            dst = out_d[c * TCH:(c + 1) * TCH, :].rearrange("(t p) d -> p t d", p=128)
            P.dma("sp", dst, buf[:])

        def rstd_of(ss, scale, bias):
            sd = small()
            P.act(sd, ss, AF.Ln, scale=scale, bias=bias)
            rs = small()
            P.act(rs, sd, AF.Exp, scale=-0.5)
            return rs

        def sigm(out, in_, nscale=-1.0, nbias=None):
            if nbias is None:
                P.act(out, in_, AF.Exp, scale=nscale)
            else:
                P.act(out, in_, AF.Exp, scale=nscale, bias=nbias)
            P.act(out, out, AF.Ln, bias=1.0)
            P.act(out, out, AF.Exp, scale=-1.0)

        def rsqrt_big(out, in_, bias):
            P.act(out, in_, AF.Ln, bias=bias)
            P.act(out, out, AF.Exp, scale=-0.5)

        def prenorm_tile(xr, xnt, t):
            xt = xr[:, t, :]
            ss = small()
            P.act(JUNK[:], xt, AF.Square, accum=ss)
            rs = rstd_of(ss, 1.0 / D, EPS)
            xb = XNB[:, 0, :]
            P.ts("dve", xb, xt, rs, ALU.mult)
            for kc in range(8):
                P.tr(PT[:, kc * 128:(kc + 1) * 128], xb[:, kc * 128:(kc + 1) * 128], IDB[:])
            P.copy("dve", xnt[:, :, t * 128:(t + 1) * 128],
                   PT[:, :].rearrange("p (k c) -> p k c", k=8))

        def load_gpost(key, l, i):
            v = dr[key][l]
            P.dma("sp", GP[:, i, :], bass.AP(v.tensor, v.offset, [[0, 128], [1, D]]))
            return GP[:, i, :]

        def epilogue2(xr, t, py0, py1, gp, fac):
            ss0 = small()
            P.act(JUNK[:, 0:512], py0, AF.Square, accum=ss0)
            ss1 = small()
            P.act(JUNK[:, 512:1024], py1, AF.Square, accum=ss1)
            ss = small()
            P.tt("dve", ss, ss0, ss1, ALU.add)
            f2 = 1.0 / (fac * fac)
            rs = rstd_of(ss, f2 / D, f2 * EPS)
            tm = TMPE[:, 0, :]
            P.stt(tm, py0, rs, gp[:, 0:512], ALU.mult, ALU.mult)
            P.tt("pool", xr[:, t, 0:512], xr[:, t, 0:512], tm, ALU.add)
            tm2 = TMPE[:, 1, :]
            P.stt(tm2, py1, rs, gp[:, 512:1024], ALU.mult, ALU.mult)
            P.tt("pool", xr[:, t, 512:1024], xr[:, t, 512:1024], tm2, ALU.add)

        GROUPS = ((0, 8), (1, 8), (2, 6))

        def ffn_gen(c, l, which, do_pre, has_next, exl):
            sid = c % 2
            xr, xnt = XR[sid], XNTS[sid]
            gp = load_gpost("ffn1_post_g" if which == 1 else "ffn2_post_g", l, 0)
            gk, dk = ("gu1", "dn1") if which == 1 else ("gu2", "dn2")
            alt = [Q[4], Q[5], Q[6], Q[0]]

            def dbank(pair, ti, hh):
                if pair == 1 and not exl:
                    return alt[ti * 2 + hh]
                return Q[ti * 2 + hh]

            if do_pre:
                for t in range(NT):
                    prenorm_tile(xr, xnt, t)
                    yield
            for s_ in range(11):
                slot, idx = next_slab((gk, l), s_)
                for jj in range(2):
                    j = 2 * s_ + jj
                    pg, pu = Q[2 * (j % 2)], Q[2 * (j % 2) + 1]
                    for kc in range(8):
                        P.mm(pg[:], slot[:, kc, jj * 128:(jj + 1) * 128], xnt[:, kc, :], kc == 0, kc == 7)
                        if kc == 3 and exl:
                            yield
                    yield
                    for kc in range(8):
                        P.mm(pu[:], slot[:, kc, 256 + jj * 128:256 + (jj + 1) * 128], xnt[:, kc, :],
                             kc == 0, kc == 7)
                        if kc == 3 and exl:
                            yield
                    sg = SGF[:, j % 2, :]
                    if exl:
                        sigm(sg, pg[:])
                        P.tt("dve", sg, sg, pg[:], ALU.mult)
                    else:
                        P.act(sg, pg[:], AF.Silu)
                    P.tt("dve", HT[:, j, :], sg, pu[:], ALU.mult)
                    yield
                release(idx)
            for pair in range(2):
                tiles = (2 * pair, 2 * pair + 1)
                for hh in range(2):
                    for g, nj in GROUPS:
                        slot, idx = next_slab((dk, l), hh * 3 + g)
                        for ti, t in enumerate(tiles):
                            bank = dbank(pair, ti, hh)
                            for jl in range(nj):
                                j = g * 8 + jl
                                P.mm(bank[:], HT[:, j, t * 128:(t + 1) * 128], slot[:, jl, :],
                                     j == 0, j == NJ - 1)
                                if jl == 3 and exl:
                                    yield
                            yield
                        release(idx)
                if pair == 1 and has_next:
                    for t in (0, 1):
                        prenorm_tile(xr, xnt, t)
                        yield
                for ti, t in enumerate(tiles):
                    epilogue2(xr, t, dbank(pair, ti, 0)[:], dbank(pair, ti, 1)[:], gp, 0.5)
                    if not exl:
                        yield
                if exl:
                    yield
            if has_next:
                for t in (2, 3):
                    prenorm_tile(xr, xnt, t)
                    yield

        def build_diag(l):
            for cc in range(2):
                for k in range(4):
                    P.ts("dve", DG[:, cc * 4 + k, :], IDB[:], par(l, "lcw", cc * 4 + k), ALU.mult)

        def mixer_gen(c, l, do_pre, has_next):
            sid = c % 2
            xr, xnt = XR[sid], XNTS[sid]
            gp = load_gpost("mix_post_g", l, 1)
            if do_pre:
                for t in range(NT):
                    prenorm_tile(xr, xnt, t)
                    yield
            B = [Q[4], Q[5], Q[6]]
            bi = [0]

            def bank():
                b_ = B[bi[0] % 3]
                bi[0] += 1
                return b_

            def proj(slot, o, ps):
                for kc in range(8):
                    P.mm(ps, slot[:, kc, o:o + 128], xnt[:, kc, :], kc == 0, kc == 7)

            GA = T2(0)
            slot, idx = next_slab(("win", l), 2)
            ps = bank()
            proj(slot, 0, ps[:])
            yield
            P.copy("act", KT[l][:, 128:128 + TCH], ps[:])
            psv = bank()
            for t in range(NT):
                for kc in range(8):
                    P.mm(psv[:, t * 128:(t + 1) * 128], xnt[:, kc, t * 128:(t + 1) * 128],
                         slot[:, kc, 128:256], kc == 0, kc == 7)
            yield
            P.copy("dve", VA[l][:, 1:NT + 1, :, 0:64],
                   psv[:, :].rearrange("p (t g d) -> p t g d", t=NT, g=2))
            for cc in range(2):
                psa = bank()
                proj(slot, 256 + cc * 128, psa[:])
                yield
                P.copy("act", GA[:, cc, :], psa[:])
            release(idx)
            slot, idx = next_slab(("win", l), 3)
            for cc in range(2):
                psg = bank()
                proj(slot, cc * 128, psg[:])
                yield
                sgm = T2(1)[:, cc, :]
                sigm(sgm, psg[:])
                P.tt("dve", CIN[l][:, cc, 30:30 + TCH], GA[:, cc, :], sgm, ALU.mult)
            release(idx)
            YCc = T2(0)
            SQc = T2(1)

            def conv_chain():
                for cc in range(2):
                    P.ts("dve", YCc[:, cc, :], CIN[l][:, cc, 0:TCH], par(l, "ccw", cc * 31), ALU.mult,
                         par(l, "ccb", cc), ALU.add)
                yield None
                for k in range(1, 31):
                    for cc in range(2):
                        acc = YCc[:, cc, :]
                        P.stt(acc, CIN[l][:, cc, k:k + TCH], par(l, "ccw", cc * 31 + k), acc, ALU.mult, ALU.add)
                    yield None
                for cc in range(2):
                    P.act(SQc[:, cc, :], YCc[:, cc, :], AF.Square)
                yield None

            cchain = conv_chain()

            def cstep():
                try:
                    next(cchain)
                except StopIteration:
                    pass

            slot, idx = next_slab(("win", l), 1)
            for cq in range(4):
                ps = bank()
                proj(slot, cq * 128, ps[:])
                cstep()
                yield
                P.copy("dve" if cq % 2 else "act", QT[:, cq, :], ps[:])
            release(idx)
            slot, idx = next_slab(("win", l), 0)
            for cc in range(2):
                ps = bank()
                proj(slot, cc * 128, ps[:])
                cstep()
                yield
                P.copy("act", LX[l][:, cc, 3:3 + TCH], ps[:])
            for cc in range(2):
                ps = bank()
                proj(slot, 256 + cc * 128, ps[:])
                cstep()
                yield
                P.copy("act", GATE[:, cc, :], ps[:])
            release(idx)
            cstep()
            yield

            YBS = [MF[:, 6, :], MF[:, 17, :]]
            NU = 2 * NT

            def scores(u):
                b, g = u // 2, u % 2
                hasprev = (c * NT + b) > 0
                prt = slice(g * 64, (g + 1) * 64)
                sc, sp = Q[4], Q[5]
                qv = QT[prt, :, b * 128:(b + 1) * 128]
                P.mm(sc[:, :].rearrange("p (h q) -> p h q", h=4),
                     KT[l][prt, 128 + b * 128:128 + (b + 1) * 128], qv, True, False)
                P.mm(sc[:, :], IDB[:], MSK[:, 0, :], False, True)
                if hasprev:
                    P.mm(sp[:, :].rearrange("p (h q) -> p h q", h=4),
                         KT[l][prt, b * 128:(b + 1) * 128], qv, True, False)
                    P.mm(sp[:, :], IDB[:], MSK[:, 1, :], False, True)

            def expu(u):
                b, g = u // 2, u % 2
                hasprev = (c * NT + b) > 0
                P.act(PTB[:, g, 0, :], Q[4][:], AF.Exp, scale=0.125)
                if hasprev:
                    P.act(PTB[:, g, 1, :], Q[5][:], AF.Exp, scale=0.125)

            def pvu(u):
                b, g = u // 2, u % 2
                hasprev = (c * NT + b) > 0
                ptc, ptp = PTB[:, g, 0, :], PTB[:, g, 1, :]
                po = Q[6]
                for hh in range(4):
                    o = po[:, hh * 65:(hh + 1) * 65]
                    if hasprev:
                        P.mm(o, ptp[:, hh * 128:(hh + 1) * 128], VA[l][:, b, g, :], True, False)
                    P.mm(o, ptc[:, hh * 128:(hh + 1) * 128], VA[l][:, b + 1, g, :], not hasprev, True)

            def normu(u):
                b, g = u // 2, u % 2
                yb = YBS[b % 2]
                pov = Q[6][:, 0:260].rearrange("p (h e) -> p h e", h=4)
                den = small(4)
                P.tt("dve", den, pov[:, :, 64], par(l, "esnk", g * 4, 4), ALU.add)
                rden = small(4)
                P.recip(rden, den)
                rb = bass.AP(rden.tensor, rden.offset, [list(rden.ap[0]), [1, 4], [0, 64]])
                P.tt("dve", yb[:, g * 256:(g + 1) * 256].rearrange("p (h d) -> p h d", h=4),
                     pov[:, :, 0:64], rb, ALU.mult)

            gn_rs = {}

            def gn_act(b):
                ss = small()
                P.act(JUNK[:, 0:512], YBS[b % 2], AF.Square, accum=ss)
                gn_rs[b] = rstd_of(ss, 1.0 / 512, EPS)

            def gn_scale(b):
                P.ts("dve", YBN, YBS[b % 2], gn_rs[b], ALU.mult)

            def tr_cp(b):
                for cq in range(4):
                    P.tr(PT[:, cq * 128:(cq + 1) * 128], YBN[:, cq * 128:(cq + 1) * 128], IDB[:])
                P.copy("dve", YT[:, 2:6, b * 128:(b + 1) * 128],
                       PT[:, 0:512].rearrange("p (k c) -> p k c", k=4))

            for s_ in range(NU + 4):
                if s_ < NU:
                    scores(s_)
                if 1 <= s_ <= NU:
                    pvu(s_ - 1)
                if s_ >= 3 and (s_ - 3) % 2 == 0 and (s_ - 3) // 2 < NT:
                    gn_act((s_ - 3) // 2)
                cstep()
                yield
                if s_ < NU:
                    expu(s_)
                if 1 <= s_ <= NU:
                    normu(s_ - 1)
                if s_ >= 3 and (s_ - 3) % 2 == 0 and (s_ - 3) // 2 < NT:
                    gn_scale((s_ - 3) // 2)
                if s_ >= 4 and (s_ - 4) % 2 == 0 and (s_ - 4) // 2 < NT:
                    tr_cp((s_ - 4) // 2)
                cstep()
                yield

            for _ in cchain:
                yield

            def confpost():
                pb = Q[4]
                mean = MF[:, 6, :]
                m2 = T(4)
                var = T(5)
                P.mm(pb[:], ONES[:], YCc[:, 0, :], True, False)
                P.mm(pb[:], ONES[:], YCc[:, 1, :], False, True)
                yield
                P.act(mean, pb[:], AF.Identity)
                P.act(m2, pb[:], AF.Square)
                yield
                P.mm(pb[:], ONES[:], SQc[:, 0, :], True, False)
                P.mm(pb[:], ONES[:], SQc[:, 1, :], False, True)
                yield
                P.tt("dve", var, pb[:], m2, ALU.subtract)
                yield
                rsqrt_big(var, var, LN_EPS)
                yield
                d = YCc
                for cc in range(2):
                    P.tt("dve", d[:, cc, :], YCc[:, cc, :], mean, ALU.subtract)
                    P.tt("dve", d[:, cc, :], d[:, cc, :], var, ALU.mult)
                    P.ts("dve", d[:, cc, :], d[:, cc, :], par(l, "lng", cc), ALU.mult, par(l, "lnb", cc), ALU.add)
                    yield
                    sgc = SQc[:, cc, :]
                    sigm(sgc, d[:, cc, :])
                    yield
                    P.tt("dve", d[:, cc, :], d[:, cc, :], sgc, ALU.mult)
                    yield
                P.act(SQc, d, AF.Square)
                yield
                P.mm(pb[:], ONES[:], SQc[:, 0, :], True, False)
                P.mm(pb[:], ONES[:], SQc[:, 1, :], False, True)
                yield
                sd = m2
                rsqrt_big(sd, pb[:], EPS)
                yield
                for cc in range(2):
                    P.tt("dve", YT[:, 6 + cc, :], d[:, cc, :], sd, ALU.mult)
                yield

            def lru():
                cps = [Q[5], Q[6]]
                for cc in range(2):
                    for k in range(4):
                        P.mm(cps[cc][:], DG[:, cc * 4 + k, :], LX[l][:, cc, k:k + TCH], k == 0, k == 3)
                yield
                xc = MF[:, 13:15, :]
                for cc in range(2):
                    P.act(xc[:, cc, :], cps[cc][:], AF.Identity, bias=par(l, "lcb", cc))
                yield
                P.copy("dve", XCB[:, :, :], xc)
                yield
                ig, r, a, h = MF[:, 15:17, :], MF[:, 18:20, :], MF[:, 20:22, :], MF[:, 22:24, :]
                for cc in range(2):
                    pr_, pi_ = Q[5], Q[6]
                    P.mm(pr_[:], WA[l][:, cc, :], XCB[:, cc, :], True, True)
                    P.mm(pi_[:], WX[l][:, cc, :], XCB[:, cc, :], True, True)
                    yield
                    sigm(r[:, cc, :], pr_[:], -1.0, par(l, "nba", cc))
                    sigm(ig[:, cc, :], pi_[:], -1.0, par(l, "nbx", cc))
                    yield
                for cc in range(2):
                    P.act(a[:, cc, :], r[:, cc, :], AF.Exp, scale=par(l, "c1", cc))
                    P.act(r[:, cc, :], r[:, cc, :], AF.Exp, scale=par(l, "c2", cc))
                s2 = r
                P.tt("dve", ig, ig, xc, ALU.mult)
                P.act(s2, s2, AF.Ln, scale=-1.0, bias=1.0)
                P.act(s2, s2, AF.Exp, scale=0.5)
                yield
                P.tt("dve", ig, ig, s2, ALU.mult)
                yield
                for cc in range(2):
                    h0 = HST[l][:, cc:cc + 1]
                    P.add("dve", lambda e, h_=h[:, cc, :], a_=a[:, cc, :], u_=ig[:, cc, :], h0=h0:
                          e.tensor_tensor_scan(h_, a_, u_, h0, ALU.mult, ALU.add),
                          [a[:, cc, :], ig[:, cc, :], h0], [h[:, cc, :]])
                    P.copy("pool", h0, h[:, cc, TCH - 1:TCH])
                w = r
                P.act(w, GATE, AF.Square)
                yield
                P.ts("dve", w, w, 0.044715, ALU.mult, 1.0, ALU.add)
                P.tt("dve", w, w, GATE, ALU.mult)
                yield
                sgg = a
                sigm(sgg, w, -1.5957691216057308)
                yield
                P.tt("dve", sgg, sgg, GATE, ALU.mult)
                P.tt("dve", YA, sgg, h, ALU.mult)
                yield
                P.act(SQ, YA, AF.Square)
                yield
                pn = Q[5]
                P.mm(pn[:], ONES[:], SQ[:, 0, :], True, False)
                P.mm(pn[:], ONES[:], SQ[:, 1, :], False, True)
                yield
                sd2 = MF[:, 13, :]
                rsqrt_big(sd2, pn[:], EPS)
                yield
                for cc in range(2):
                    P.tt("dve", YT[:, cc, :], YA[:, cc, :], sd2, ALU.mult)
                yield

            subs = [lru(), confpost()]
            while subs:
                for g_ in list(subs):
                    try:
                        next(g_)
                    except StopIteration:
                        subs.remove(g_)
                yield

            so0, io0 = next_slab(("wout", l), 0)
            so1, io1 = next_slab(("wout", l), 1)
            ob = [(Q[4], Q[5]), (Q[6], Q[4]), (Q[5], Q[6]), (Q[4], Q[5])]
            for t in range(NT):
                b0, b1 = ob[t]
                for dc in range(8):
                    P.mm(b0[:], YT[:, dc, t * 128:(t + 1) * 128], so0[:, dc, :], dc == 0, dc == 7)
                for dc in range(8):
                    P.mm(b1[:], YT[:, dc, t * 128:(t + 1) * 128], so1[:, dc, :], dc == 0, dc == 7)
                yield
                if has_next and t >= 1:
                    prenorm_tile(xr, xnt, t - 1)
                epilogue2(xr, t, b0[:], b1[:], gp, 1.0)
                yield
            release(io0)
            release(io1)
            if has_next:
                prenorm_tile(xr, xnt, NT - 1)
                yield
            P.copy("pool", KT[l][:, 0:128], KT[l][:, TCH:TCH + 128])
            P.copy("pool", VA[l][:, 0, :, :], VA[l][:, NT, :, :])
            P.copy("pool", LX[l][:, :, 0:3], LX[l][:, :, TCH:TCH + 3])
            P.copy("pool", CIN[l][:, :, 0:30], CIN[l][:, :, TCH:TCH + 30])

        step_counts = {}

        def run_slot(items):
            st_ = [[name, kind, gen, 0] for name, kind, gen in items]
            while st_:
                st_.sort(key=lambda it: (it[3] + 1) / float(step_counts.get(it[0], 1 << 30)))
                it = st_[0]
                cur_k[0] = it[1]
                try:
                    next(it[2])
                    it[3] += 1
                except StopIteration:
                    if P.dry:
                        step_counts[it[0]] = max(step_counts.get(it[0], 0), it[3])
                    st_.remove(it)

        seqs = [(l, st) for l in range(depth) for st in stages]
        nst = len(seqs)

        def make(c, k, overl):
            l, st = seqs[k]
            do_pre = k == 0
            has_next = k + 1 < nst
            if st == "mix":
                return ("mix", 1, mixer_gen(c, l, do_pre, has_next))
            nm = "ffx" if overl else "ffn"
            return (nm, 0, ffn_gen(c, l, 1 if st == "f1" else 2, do_pre, has_next, overl))

        two_stream = (depth == 2 and tuple(stages) == ("f1", "mix", "f2"))

        def record_main():
            ring_reset()
            if two_stream:
                load_x(0)
                if nch > 1:
                    load_x(1)
                for c in range(nch + 1):
                    items = []
                    if c >= 1:
                        build_diag(1)
                        items.append(make(c - 1, 4, False))
                    if c < nch:
                        items.append(make(c, 0, c >= 1))
                    run_slot(items)
                    items = []
                    if c < nch:
                        build_diag(0)
                        items.append(make(c, 1, False))
                    if c >= 1:
                        items.append(make(c - 1, 5, c < nch))
                    run_slot(items)
                    if c >= 1:
                        store_x(c - 1)
                        if c + 1 < nch:
                            load_x(c + 1)
                    if c < nch:
                        run_slot([make(c, 2, False)])
                        run_slot([make(c, 3, False)])
            else:
                load_x(0)
                for c in range(nch):
                    if c + 1 < nch:
                        load_x(c + 1)
                    for k, (l, st) in enumerate(seqs):
                        if st == "mix":
                            build_diag(l)
                        run_slot([make(c, k, False)])
                    store_x(c)

        P.dry = True
        record_main()
        ring["plan"] = []
        frozen = dict(step_counts)
        record_main()
        assert frozen == step_counts
        P.dry = False
        record_main()
        P.add("pool", lambda e: e.memset(SM[:, 0:1], 0.0), [out_d[:, :]], [SM[:, 0:1]])
        P.emit_all(es)
    return nc


_CACHE = {}


def kernel(**inputs):
    if "nc" not in _CACHE:
        _CACHE["nc"] = build()
    nc = _CACHE["nc"]
    x = np.ascontiguousarray(inputs["x"], dtype=np.float32)
    ws = {n: np.ascontiguousarray(inputs[n], dtype=np.float32) for n in WNAMES}
    in_maps = []
    for b in range(8):
        m = {"x": x[b]}
        m.update(ws)
        in_maps.append(m)
    res = run_bass_kernel_spmd(nc, in_maps, core_ids=list(range(8)))
    return np.stack([r["out"] for r in res.results], axis=0)
```

```python
import math
from contextlib import ExitStack

import numpy as np
import concourse.bass as bass
import concourse.mybir as mybir
from concourse.bass_utils import run_bass_kernel_spmd

F32 = mybir.dt.float32
BF16 = mybir.dt.bfloat16
AF = mybir.ActivationFunctionType
ALU = mybir.AluOpType

D = 1024
SEQ = 4096
DFF = 2816
NJ = 22
DIN = 1792
TCH = 512
NT = TCH // 128
NCH = SEQ // TCH
NSLOT = 5
SLAB = 4096
EPS = 1e-6
LN_EPS = 1e-5

ENGS = ("sp", "act", "dve", "pool", "pe")


class Inst:
    __slots__ = ("eng", "emit", "dma", "seq", "signal", "waits", "snap", "id",
                 "semi", "semv", "prewait")

    def __init__(self, eng, emit, dma):
        self.eng = eng
        self.emit = emit
        self.dma = dma
        self.signal = False
        self.waits = []
        self.prewait = None


def _merge(iv):
    iv.sort()
    out = [list(iv[0])]
    for a, b in iv[1:]:
        if a <= out[-1][1]:
            if b > out[-1][1]:
                out[-1][1] = b
        else:
            out.append([a, b])
    return out


def region(ap):
    t = ap.tensor
    name = t.name
    esz = mybir.dt.size(ap.dtype)
    dims = [list(d) for d in ap.ap]
    space = str(type(t).__name__)
    if "DRam" in space:
        return (name, 0, 1, [[0, 1 << 60]], 0, 1 << 60, True)
    pstep, pcnt = dims[0]
    off = ap.offset
    if pstep == 0:
        p0, col = 0, off
        pcnt = 1
    else:
        p0, col = off // pstep, off % pstep
    free = [d for d in dims[1:] if d[1] > 1 and d[0] != 0]
    free.sort(key=lambda d: -abs(d[0]))
    if not free:
        iv = [[col, col + 1]]
    else:
        inner = free[-1]
        outer = free[:-1]
        n_outer = 1
        for d in outer:
            n_outer *= d[1]
        if inner[0] != 1 or n_outer > 256:
            lo = col + sum(min(0, d[0] * (d[1] - 1)) for d in free)
            hi = col + sum(max(0, d[0] * (d[1] - 1)) for d in free) + 1
            iv = [[lo, hi]]
        else:
            starts = [col]
            for d in outer:
                starts = [s + d[0] * i for s in starts for i in range(d[1])]
            iv = _merge([[s, s + inner[1]] for s in starts])
    iv = [[a * esz, b * esz] for a, b in iv]
    return (name, p0, p0 + pcnt, iv, iv[0][0], iv[-1][1], False)


def _overlap(r, q):
    if r[2] <= q[1] or q[2] <= r[1] or r[5] <= q[4] or q[5] <= r[4]:
        return False
    a, b = r[3], q[3]
    i = j = 0
    while i < len(a) and j < len(b):
        if a[i][1] <= b[j][0]:
            i += 1
        elif b[j][1] <= a[i][0]:
            j += 1
        else:
            return True
    return False


def _contains(r, q):
    if q[1] < r[1] or q[2] > r[2] or q[4] < r[4] or q[5] > r[5]:
        return False
    a = r[3]
    for s, e in q[3]:
        ok = False
        for s2, e2 in a:
            if s2 <= s and e <= e2:
                ok = True
                break
        if not ok:
            return False
    return True


class Prog:
    NDSEM = 8
    EPOCH = 30000

    def __init__(self, nc):
        self.nc = nc
        self.lists = {e: [] for e in ENGS}
        self.recs = {}
        self.known = {e: {f: -1 for f in ENGS} for e in ENGS}
        self.known_dma = {e: set() for e in ENGS}
        self.dma_hist = {e: [] for e in ENGS}
        self.nid = 0
        self.dry = False

    def add(self, eng, emit, reads, writes, dma=False):
        if self.dry:
            return None
        inst = Inst(eng, emit, dma)
        inst.id = self.nid
        self.nid += 1
        inst.seq = len(self.lists[eng])
        deps = {}
        rregs = [region(a) for a in reads]
        wregs = [r_ for r_ in (region(a) for a in writes) if r_[0] != "junk"]
        for r in rregs:
            for rec in self.recs.get(r[0], ()):
                if rec[1] and _overlap(r, rec[0]):
                    deps[rec[2].id] = (rec[2], True)
        for w in wregs:
            for rec in self.recs.get(w[0], ()):
                if w[6] and rec[1]:
                    continue
                if _overlap(w, rec[0]):
                    if rec[2].id not in deps:
                        deps[rec[2].id] = (rec[2], False)
        kn = self.known[eng]
        for jid in sorted(deps):
            J, raw = deps[jid]
            if J.dma:
                if J.id in self.known_dma[eng]:
                    continue
                J.signal = True
                inst.waits.append(J)
                self.known_dma[eng].add(J.id)
                for f, v in J.snap.items():
                    if v > kn[f]:
                        kn[f] = v
            else:
                F = J.eng
                if F == eng and F == "pe":
                    continue
                if kn[F] >= J.seq:
                    continue
                J.signal = True
                inst.waits.append(J)
                kn[F] = J.seq
                for f, v in J.snap.items():
                    if v > kn[f]:
                        kn[f] = v
        if dma:
            h = self.dma_hist[eng]
            if len(h) >= self.NDSEM:
                prev = h[len(h) - self.NDSEM]
                prev.signal = True
                inst.prewait = prev
                self.known_dma[eng].add(prev.id)
            h.append(inst)
        inst.snap = dict(kn)
        for r in rregs:
            lst = self.recs.setdefault(r[0], [])
            done = False
            if not dma:
                for rec in lst:
                    if (not rec[1]) and rec[2].eng == eng and not rec[2].dma \
                            and rec[0][1:4] == r[1:4]:
                        rec[2] = inst
                        done = True
                        break
            if not done:
                lst.append([r, False, inst])
        for w in wregs:
            lst = self.recs.setdefault(w[0], [])
            if w[6]:
                lst.append([w, True, inst])
            else:
                lst[:] = [rec for rec in lst if not _contains(w, rec[0])]
                lst.append([w, True, inst])
        self.lists[eng].append(inst)
        return inst

    def emit_all(self, es):
        nc = self.nc
        sems = {}
        for e in ENGS:
            n = 0
            for inst in self.lists[e]:
                if inst.dma or not inst.signal:
                    continue
                inst.semi = (e, n // self.EPOCH)
                inst.semv = n % self.EPOCH + 1
                n += 1
            nep = n // self.EPOCH + 1
            for k in range(nep):
                sems[(e, k)] = es.enter_context(nc.semaphore(f"c_{e}_{k}"))
            h = self.dma_hist[e]
            for i, inst in enumerate(h):
                inst.semi = (e, "d", i % self.NDSEM)
                inst.semv = 16 * (i // self.NDSEM + 1)
            if h:
                for k in range(self.NDSEM):
                    sems[(e, "d", k)] = es.enter_context(nc.semaphore(f"d_{e}_{k}"))
        lists = self.lists

        def run(engname, eng):
            for inst in lists[engname]:
                if inst.prewait is not None:
                    p = inst.prewait
                    eng.wait_ge(sems[p.semi], p.semv)
                for J in inst.waits:
                    eng.wait_ge(sems[J.semi], J.semv)
                bi = inst.emit(eng)
                if inst.dma:
                    bi.then_inc(sems[inst.semi], 16)
                elif inst.signal:
                    bi.then_inc(sems[inst.semi], 1)

        with nc.Block() as block:
            @block.sync
            def _(e):
                run("sp", e)

            @block.scalar
            def _(e):
                run("act", e)

            @block.vector
            def _(e):
                run("dve", e)

            @block.gpsimd
            def _(e):
                run("pool", e)

            @block.tensor
            def _(e):
                run("pe", e)

    def act(self, out, in_, func, bias=None, scale=None, accum=None):
        reads = [in_]
        kw = {}
        if bias is not None:
            kw["bias"] = bias
            if not isinstance(bias, (int, float)):
                reads.append(bias)
        if scale is not None:
            kw["scale"] = scale
            if not isinstance(scale, (int, float)):
                reads.append(scale)
        writes = [out]
        if accum is not None:
            kw["accum_out"] = accum
            writes.append(accum)
        return self.add("act", lambda e: e.activation(out, in_, func, **kw), reads, writes)

    def tt(self, eng, out, a, b, op):
        return self.add(eng, lambda e: e.tensor_tensor(out, a, b, op), [a, b], [out])

    def ts(self, eng, out, a, s1, op0, s2=None, op1=None):
        reads = [a]
        if not isinstance(s1, (int, float)):
            reads.append(s1)
        if s2 is not None and not isinstance(s2, (int, float)):
            reads.append(s2)
        if op1 is None:
            return self.add(eng, lambda e: e.tensor_scalar(out, a, s1, None, op0), reads, [out])
        return self.add(eng, lambda e: e.tensor_scalar(out, a, s1, s2, op0, op1), reads, [out])

    def stt(self, out, in0, scalar, in1, op0, op1):
        reads = [in0, in1]
        if not isinstance(scalar, (int, float)):
            reads.append(scalar)
        return self.add("dve", lambda e: e.scalar_tensor_tensor(out, in0, scalar, in1, op0, op1),
                        reads, [out])

    def copy(self, eng, out, in_):
        if eng == "act":
            return self.add("act", lambda e: e.activation(out, in_, AF.Copy), [in_], [out])
        return self.add(eng, lambda e: e.tensor_copy(out, in_), [in_], [out])

    def recip(self, out, in_):
        return self.add("dve", lambda e: e.reciprocal(out, in_), [in_], [out])

    def memset(self, eng, out, val):
        return self.add(eng, lambda e: e.memset(out, val), [], [out])

    def mm(self, out, lhsT, rhs, start, stop):
        return self.add("pe", lambda e: e.matmul(out, lhsT, rhs, start=start, stop=stop),
                        [lhsT, rhs], [out])

    def tr(self, out, in_, ident):
        return self.add("pe", lambda e: e.transpose(out, in_, ident), [in_, ident], [out])

    def dma(self, q, out, in_, **kw):
        return self.add(q, lambda e: e.dma_start(out, in_, **kw), [in_], [out], dma=True)


WNAMES = ["ffn1_pre_g", "ffn1_w_gu", "ffn1_w_down", "ffn1_post_g", "mix_pre_g", "w_in",
          "lru_conv_w", "lru_conv_b", "lru_w_a", "lru_b_a", "lru_w_x", "lru_b_x",
          "lru_lambda", "attn_sinks", "conv_w", "conv_b", "conv_ln_g", "conv_ln_b",
          "group_g", "w_out", "mix_post_g", "ffn2_pre_g", "ffn2_w_gu", "ffn2_w_down",
          "ffn2_post_g"]
WSHAPES = {
    "ffn1_pre_g": (2, 1024), "ffn1_w_gu": (2, 1024, 5632), "ffn1_w_down": (2, 2816, 1024),
    "ffn1_post_g": (2, 1024), "mix_pre_g": (2, 1024), "w_in": (2, 1024, 1792),
    "lru_conv_w": (2, 4, 256), "lru_conv_b": (2, 256), "lru_w_a": (2, 4, 64, 64),
    "lru_b_a": (2, 256), "lru_w_x": (2, 4, 64, 64), "lru_b_x": (2, 256),
    "lru_lambda": (2, 256), "attn_sinks": (2, 8), "conv_w": (2, 31, 256), "conv_b": (2, 256),
    "conv_ln_g": (2, 256), "conv_ln_b": (2, 256), "group_g": (2, 1024), "w_out": (2, 1024, 1024),
    "mix_post_g": (2, 1024), "ffn2_pre_g": (2, 1024), "ffn2_w_gu": (2, 1024, 5632),
    "ffn2_w_down": (2, 2816, 1024), "ffn2_post_g": (2, 1024),
}

PC = {}
_o = 0
for _n, _w in [("g1", 8), ("gm", 8), ("g2", 8), ("gg", 8), ("lcw", 8), ("lcb", 2), ("ba", 2),
               ("bx", 2), ("lam", 2), ("c1", 2), ("c2", 2), ("nba", 2), ("nbx", 2), ("ccw", 62), ("ccb", 2),
               ("lng", 2), ("lnb", 2), ("snk", 8), ("esnk", 8)]:
    PC[_n] = (_o, _w)
    _o += _w
NPAR = _o


def build(depth=2, stages=("f1", "mix", "f2"), nch=NCH, seq=SEQ):
    nc = bass.Bass("TRN2", target_bir_lowering=False, dynamic_dma_scratch_size=512)
    P = Prog(nc)
    dr = {}
    x_d = nc.dram_tensor("x", [seq, D], F32, kind="ExternalInput").ap()
    for n in WNAMES:
        dr[n] = nc.dram_tensor(n, list(WSHAPES[n]), F32, kind="ExternalInput").ap()
    out_d = nc.dram_tensor("out", [seq, D], F32, kind="ExternalOutput").ap()
    scr = {}
    for l in range(depth):
        for nm, ns in (("gu1", 11), ("dn1", 6), ("win", 4), ("wout", 2), ("gu2", 11), ("dn2", 6)):
            scr[(nm, l)] = nc.dram_tensor(f"scr_{nm}_{l}", [ns, 128, SLAB], BF16, kind="Internal").ap()

    with ExitStack() as es:
        def sb(name, shape, dt):
            return es.enter_context(nc.sbuf_tensor(name, shape, dt))

        def pst(name, shape, dt):
            return es.enter_context(nc.psum_tensor(name, shape, dt))

        XR = [sb(f"xres{i}", [128, NT, D], F32) for i in range(2)]
        XNTS = [sb(f"xnT{i}", [128, 8, TCH], BF16) for i in range(2)]
        MS = sb("ms", [128, 9728], BF16)
        U = sb("U", [128, NSLOT * SLAB + NJ * TCH], BF16)
        GP = sb("gpost", [128, 2, D], F32)
        PAR = sb("par", [128, 2, NPAR], F32)
        IDB = sb("identb", [128, 128], BF16)
        ONES = sb("ones32", [128, 128], F32)
        XNB = sb("xnb", [128, 1, D], BF16)
        JUNK = sb("junk", [128, D], BF16)
        SM = sb("small", [128, 128], F32)
        SGF = sb("sgf", [128, 2, TCH], F32)
        MF = sb("mf", [128, 24, 512], F32)
        IDF = MF[:, 0, 0:128]
        YB0 = MF[:, 7:11, :]
        DG = sb("diag", [128, 8, 128], BF16)
        KT = [sb(f"kT{l}", [128, 128 + TCH], BF16) for l in range(depth)]
        VA = [sb(f"va{l}", [128, NT + 1, 2, 65], BF16) for l in range(depth)]
        LX = [sb(f"lx{l}", [128, 2, 3 + TCH], BF16) for l in range(depth)]
        CIN = [sb(f"cin{l}", [128, 2, 30 + TCH], BF16) for l in range(depth)]
        HST = [sb(f"hst{l}", [128, 2], F32) for l in range(depth)]
        WA = [sb(f"wa{l}", [128, 2, 128], BF16) for l in range(depth)]
        WX = [sb(f"wx{l}", [128, 2, 128], BF16) for l in range(depth)]
        TMPE = sb("tmpe", [128, 2, 512], F32)
        MSK = sb("msk", [128, 2, 512], BF16)
        Q = [pst(f"ps{i}", [128, 512], F32) for i in range(7)]
        PT = pst("pst", [128, 1024], BF16)

        slots = [U[:, i * SLAB:(i + 1) * SLAB].rearrange("p (a b) -> p a b", a=8) for i in range(NSLOT)]
        HT = U[:, NSLOT * SLAB:NSLOT * SLAB + NJ * TCH].rearrange("p (a b) -> p a b", a=NJ)
        QT = MS[:, 0:2048].rearrange("p (a b) -> p a b", a=4)
        PTB = MS[:, 2048:4096].rearrange("p (g a b) -> p g a b", g=2, a=2)
        YT = MS[:, 4096:8192].rearrange("p (a b) -> p a b", a=8)
        YBN = MS[:, 8192:8704]
        XCB = MS[:, 8704:9728].rearrange("p (a b) -> p a b", a=2)
        GATE = MF[:, 0:2, :]
        YC = MF[:, 0:2, :]
        YA = MF[:, 2:4, :]
        ZZ = MF[:, 2:4, :]
        SQ = MF[:, 4:6, :]
        YB = MF[:, 6, :]

        def T(i):
            return MF[:, 7 + i, :]

        def T2(i):
            return MF[:, 7 + 2 * i:9 + 2 * i, :]

        ST32 = [U[:, i * 11264:(i + 1) * 11264].bitcast(F32) for i in range(2)]
        ST16 = [U[:, 22528:28160], MS[:, 0:5632]]

        smc = [0, 0]
        cur_k = [0]

        def small(n=1):
            k = cur_k[0]
            c = smc[k]
            if c + n > 64:
                c = 0
            smc[k] = c + n
            return SM[:, k * 64 + c:k * 64 + c + n]

        def par(l, name, i=0, n=1):
            o, w = PC[name]
            return PAR[:, l, o + i:o + i + n]

        P.memset("pool", IDF[:], 0.0)
        P.add("pool", lambda e: e.affine_select(IDF[:], IDF[:], [[1, 128]], ALU.not_equal, 1.0,
                                                base=0, channel_multiplier=-1), [IDF[:]], [IDF[:]])
        P.copy("dve", IDB[:], IDF[:])
        P.memset("pool", MSK[:], 0.0)
        P.add("pool", lambda e: e.affine_select(MSK[:, 0, :], MSK[:, 0, :], [[0, 4], [1, 128]], ALU.is_ge, -30000.0,
                                                base=0, channel_multiplier=-1), [MSK[:, 0, :]], [MSK[:, 0, :]])
        P.add("pool", lambda e: e.affine_select(MSK[:, 1, :], MSK[:, 1, :], [[0, 4], [-1, 128]], ALU.is_gt, -30000.0,
                                                base=0, channel_multiplier=1), [MSK[:, 1, :]], [MSK[:, 1, :]])
        P.memset("dve", ONES[:], 1.0 / 256.0)

        def load_cols(dst, vec, ncols):
            src = bass.AP(vec.tensor, vec.offset, [[1, 128], [128, ncols]])
            P.dma("sp", dst, src, allow_slow_non_contiguous=True)

        for l in range(depth):
            for nm, key in (("g1", "ffn1_pre_g"), ("gm", "mix_pre_g"), ("g2", "ffn2_pre_g"),
                            ("gg", "group_g")):
                load_cols(par(l, nm, 0, 8), dr[key][l], 8)
            for nm, key in (("lcb", "lru_conv_b"), ("ba", "lru_b_a"), ("bx", "lru_b_x"),
                            ("lam", "lru_lambda"), ("ccb", "conv_b"), ("lng", "conv_ln_g"),
                            ("lnb", "conv_ln_b")):
                load_cols(par(l, nm, 0, 2), dr[key][l], 2)
            for nm, key, K in (("lcw", "lru_conv_w", 4), ("ccw", "conv_w", 31)):
                v = dr[key][l]
                for cc in range(2):
                    src = bass.AP(v.tensor, v.offset + cc * 128, [[1, 128], [256, K]])
                    P.dma("sp", par(l, nm, cc * K, K), src, allow_slow_non_contiguous=True)
            v = dr["attn_sinks"][l]
            P.dma("sp", par(l, "snk", 0, 8), bass.AP(v.tensor, v.offset, [[0, 128], [1, 8]]))
            t1 = small(2)
            P.act(t1, par(l, "lam", 0, 2), AF.Exp, scale=-1.0)
            t2 = small(2)
            P.act(t2, t1, AF.Ln, bias=1.0)
            P.ts("dve", par(l, "c1", 0, 2), t2, -8.0, ALU.mult)
            P.ts("dve", par(l, "c2", 0, 2), t2, -16.0, ALU.mult)
            P.act(par(l, "esnk", 0, 8), par(l, "snk", 0, 8), AF.Exp)
            P.ts("dve", par(l, "nba", 0, 2), par(l, "ba", 0, 2), -1.0, ALU.mult)
            P.ts("dve", par(l, "nbx", 0, 2), par(l, "bx", 0, 2), -1.0, ALU.mult)

        if "mix" in stages:
            for l in range(depth):
                P.memset("pool", LX[l][:, :, 0:3], 0.0)
                P.memset("pool", CIN[l][:, :, 0:30], 0.0)
                P.memset("pool", HST[l][:], 0.0)
                P.memset("pool", VA[l][:], 1.0)
                for wt, key in ((WA, "lru_w_a"), (WX, "lru_w_x")):
                    stg = MF[:, 0:1, 0:256].rearrange("p a (c q) -> p (a c) q", c=2)
                    P.memset("dve", stg, 0.0)
                    for cc in range(2):
                        for hh in range(2):
                            P.dma("sp", stg[hh * 64:(hh + 1) * 64, cc, hh * 64:(hh + 1) * 64],
                                  dr[key][l][2 * cc + hh])
                    P.copy("dve", wt[l][:], stg)

        cast_rr = [0]

        def cast(out, in_, scal):
            e = ("dve", "act")[cast_rr[0] % 2]
            cast_rr[0] += 1
            if e == "act":
                if scal is None:
                    P.copy("act", out, in_)
                else:
                    P.act(out, in_, AF.Copy, scale=scal)
            else:
                if scal is None:
                    P.copy(e, out, in_)
                else:
                    P.ts(e, out, in_, scal, ALU.mult)

        stc = [0]

        def stage_bufs():
            i = stc[0] % 2
            stc[0] += 1
            return ST32[i], ST16[i]

        def conv_gu(l, key, gname, sname):
            w = dr[key][l]
            dst = scr[(sname, l)].rearrange("s p (k c) -> p s k c", k=8)
            for kc in range(8):
                s32, s16 = stage_bufs()
                P.dma("sp", s32[:, :], w[kc * 128:(kc + 1) * 128, :])
                o = s16[:, :].rearrange("p (s h c) -> p s h c", s=11, h=2)
                i_ = s32[:, :].rearrange("p (h s c) -> p s h c", h=2, s=11)
                for h in range(2):
                    cast(o[:, :, h, :], i_[:, :, h, :], par(l, gname, kc))
                P.dma("sp", dst[:, :, kc, :], s16[:, :].rearrange("p (s c) -> p s c", s=11))

        def conv_dn(l, key, sname):
            w = dr[key][l].rearrange("(j p) c -> p j c", p=128)
            dst = scr[(sname, l)].rearrange("s p (j c) -> p s j c", j=8)
            groups = [(0, 0, 4), (0, 4, 4), (1, 0, 4), (1, 4, 4), (2, 0, 3), (2, 3, 3)]
            for g, j0, nj in groups:
                jg = g * 8 + j0
                s32, s16 = stage_bufs()
                P.dma("sp", s32[:, 0:nj * 1024].rearrange("p (j c) -> p j c", j=nj), w[:, jg:jg + nj, :])
                cast(s16[:, 0:nj * 1024], s32[:, 0:nj * 1024], None)
                v16 = s16[:, 0:nj * 1024].rearrange("p (j c) -> p j c", j=nj)
                for hh in range(2):
                    P.dma("sp", dst[:, hh * 3 + g, j0:j0 + nj, :], v16[:, :, hh * 512:(hh + 1) * 512])

        def conv_win(l):
            w = dr["w_in"][l]
            dst = scr[("win", l)].rearrange("s p (k c) -> p s k c", k=8)
            for kc in range(8):
                s32, s16 = stage_bufs()
                P.dma("sp", s32[:, 0:DIN], w[kc * 128:(kc + 1) * 128, :])
                g = par(l, "gm", kc)
                cast(s16[:, 0:512], s32[:, 0:512], g)
                o = s16[:, 512:1024].rearrange("p (c h d) -> p c h d", c=4, h=2)
                i_ = s32[:, 512:1024].rearrange("p (h c d) -> p c h d", h=2, c=4)
                for h in range(2):
                    cast(o[:, :, h, :], i_[:, :, h, :], g)
                cast(s16[:, 1024:DIN], s32[:, 1024:DIN], g)
                P.memset("pool", s16[:, DIN:2048], 0.0)
                P.dma("sp", dst[:, :, kc, :], s16[:, 0:2048].rearrange("p (s c) -> p s c", s=4))

        def conv_wout(l):
            w = dr["w_out"][l]
            dst = scr[("wout", l)].rearrange("s p (k c) -> p s k c", k=8)
            for dc in range(8):
                s32, s16 = stage_bufs()
                P.dma("sp", s32[:, 0:1024], w[dc * 128:(dc + 1) * 128, :])
                cast(s16[:, 0:1024], s32[:, 0:1024], par(l, "gg", dc))
                P.dma("sp", dst[:, :, dc, :], s16[:, 0:1024].rearrange("p (s c) -> p s c", s=2))

        for l in range(depth):
            if "f1" in stages:
                conv_gu(l, "ffn1_w_gu", "g1", "gu1")
                conv_dn(l, "ffn1_w_down", "dn1")
            if "mix" in stages:
                conv_win(l)
                conv_wout(l)
            if "f2" in stages:
                conv_gu(l, "ffn2_w_gu", "g2", "gu2")
                conv_dn(l, "ffn2_w_down", "dn2")

        ring = {"plan": [], "issued": 0, "next": 0, "out": set()}

        def ring_reset():
            ring["issued"] = 0
            ring["next"] = 0
            ring["out"] = set()
            smc[0] = smc[1] = 0

        def issue_to(n):
            plan = ring["plan"]
            while ring["issued"] < min(n, len(plan)):
                i = ring["issued"]
                key, si_ = plan[i]
                nval = 6 * 512 if (key[0].startswith("dn") and si_ % 3 == 2) else SLAB
                P.dma("sp", U[:, (i % NSLOT) * SLAB:(i % NSLOT) * SLAB + nval], scr[key][si_][:, 0:nval])
                ring["issued"] += 1

        def next_slab(key, si_):
            i = ring["next"]
            ring["next"] += 1
            if P.dry:
                ring["plan"].append((key, si_))
                return slots[0], i
            assert ring["plan"][i] == (key, si_), (i, ring["plan"][i], key, si_)
            lo = min(ring["out"]) if ring["out"] else i
            lim = min(i + NSLOT - 1, lo + NSLOT)
            assert lim >= i + 1, (i, lo)
            issue_to(lim)
            ring["out"].add(i)
            return slots[i % NSLOT], i

        def release(i):
            ring["out"].discard(i)

        def load_x(c):
            buf = XR[c % 2]
            src = x_d[c * TCH:(c + 1) * TCH, :].rearrange("(t p) d -> p t d", p=128)
            P.dma("sp", buf[:], src)

        def store_x(c):
            buf = XR[c % 2]
            dst = out_d[c * TCH:(c + 1) * TCH, :].rearrange("(t p) d -> p t d", p=128)
            P.dma("sp", dst, buf[:])

        def rstd_of(ss, scale, bias):
            sd = small()
            P.act(sd, ss, AF.Ln, scale=scale, bias=bias)
            rs = small()
            P.act(rs, sd, AF.Exp, scale=-0.5)
            return rs

        def sigm(out, in_, nscale=-1.0, nbias=None):
            if nbias is None:
                P.act(out, in_, AF.Exp, scale=nscale)
            else:
                P.act(out, in_, AF.Exp, scale=nscale, bias=nbias)
            P.act(out, out, AF.Ln, bias=1.0)
            P.act(out, out, AF.Exp, scale=-1.0)

        def rsqrt_big(out, in_, bias):
            P.act(out, in_, AF.Ln, bias=bias)
            P.act(out, out, AF.Exp, scale=-0.5)

        def prenorm_tile(xr, xnt, t):
            xt = xr[:, t, :]
            ss = small()
            P.act(JUNK[:], xt, AF.Square, accum=ss)
            rs = rstd_of(ss, 1.0 / D, EPS)
            xb = XNB[:, 0, :]
            P.ts("dve", xb, xt, rs, ALU.mult)
            for kc in range(8):
                P.tr(PT[:, kc * 128:(kc + 1) * 128], xb[:, kc * 128:(kc + 1) * 128], IDB[:])
            P.copy("dve", xnt[:, :, t * 128:(t + 1) * 128],
                   PT[:, :].rearrange("p (k c) -> p k c", k=8))

        def load_gpost(key, l, i):
            v = dr[key][l]
            P.dma("sp", GP[:, i, :], bass.AP(v.tensor, v.offset, [[0, 128], [1, D]]))
            return GP[:, i, :]

        def epilogue2(xr, t, py0, py1, gp, fac):
            ss0 = small()
            P.act(JUNK[:, 0:512], py0, AF.Square, accum=ss0)
            ss1 = small()
            P.act(JUNK[:, 512:1024], py1, AF.Square, accum=ss1)
            ss = small()
            P.tt("dve", ss, ss0, ss1, ALU.add)
            f2 = 1.0 / (fac * fac)
            rs = rstd_of(ss, f2 / D, f2 * EPS)
            tm = TMPE[:, 0, :]
            P.stt(tm, py0, rs, gp[:, 0:512], ALU.mult, ALU.mult)
            P.tt("pool", xr[:, t, 0:512], xr[:, t, 0:512], tm, ALU.add)
            tm2 = TMPE[:, 1, :]
            P.stt(tm2, py1, rs, gp[:, 512:1024], ALU.mult, ALU.mult)
            P.tt("pool", xr[:, t, 512:1024], xr[:, t, 512:1024], tm2, ALU.add)

        GROUPS = ((0, 8), (1, 8), (2, 6))

        def ffn_gen(c, l, which, do_pre, has_next, exl):
            sid = c % 2
            xr, xnt = XR[sid], XNTS[sid]
            gp = load_gpost("ffn1_post_g" if which == 1 else "ffn2_post_g", l, 0)
            gk, dk = ("gu1", "dn1") if which == 1 else ("gu2", "dn2")
            alt = [Q[4], Q[5], Q[6], Q[0]]

            def dbank(pair, ti, hh):
                if pair == 1 and not exl:
                    return alt[ti * 2 + hh]
                return Q[ti * 2 + hh]

            if do_pre:
                for t in range(NT):
                    prenorm_tile(xr, xnt, t)
                    yield
            for s_ in range(11):
                slot, idx = next_slab((gk, l), s_)
                for jj in range(2):
                    j = 2 * s_ + jj
                    pg, pu = Q[2 * (j % 2)], Q[2 * (j % 2) + 1]
                    for kc in range(8):
                        P.mm(pg[:], slot[:, kc, jj * 128:(jj + 1) * 128], xnt[:, kc, :], kc == 0, kc == 7)
                        if kc == 3 and exl:
                            yield
                    yield
                    for kc in range(8):
                        P.mm(pu[:], slot[:, kc, 256 + jj * 128:256 + (jj + 1) * 128], xnt[:, kc, :],
                             kc == 0, kc == 7)
                        if kc == 3 and exl:
                            yield
                    sg = SGF[:, j % 2, :]
                    if exl:
                        sigm(sg, pg[:])
                        P.tt("dve", sg, sg, pg[:], ALU.mult)
                    else:
                        P.act(sg, pg[:], AF.Silu)
                    P.tt("dve", HT[:, j, :], sg, pu[:], ALU.mult)
                    yield
                release(idx)
            for pair in range(2):
                tiles = (2 * pair, 2 * pair + 1)
                for hh in range(2):
                    for g, nj in GROUPS:
                        slot, idx = next_slab((dk, l), hh * 3 + g)
                        for ti, t in enumerate(tiles):
                            bank = dbank(pair, ti, hh)
                            for jl in range(nj):
                                j = g * 8 + jl
                                P.mm(bank[:], HT[:, j, t * 128:(t + 1) * 128], slot[:, jl, :],
                                     j == 0, j == NJ - 1)
                                if jl == 3 and exl:
                                    yield
                            yield
                        release(idx)
                if pair == 1 and has_next:
                    for t in (0, 1):
                        prenorm_tile(xr, xnt, t)
                        yield
                for ti, t in enumerate(tiles):
                    epilogue2(xr, t, dbank(pair, ti, 0)[:], dbank(pair, ti, 1)[:], gp, 0.5)
                    if not exl:
                        yield
                if exl:
                    yield
            if has_next:
                for t in (2, 3):
                    prenorm_tile(xr, xnt, t)
                    yield

        def build_diag(l):
            for cc in range(2):
                for k in range(4):
                    P.ts("dve", DG[:, cc * 4 + k, :], IDB[:], par(l, "lcw", cc * 4 + k), ALU.mult)

        def mixer_gen(c, l, do_pre, has_next):
            sid = c % 2
            xr, xnt = XR[sid], XNTS[sid]
            gp = load_gpost("mix_post_g", l, 1)
            if do_pre:
                for t in range(NT):
                    prenorm_tile(xr, xnt, t)
                    yield
            B = [Q[4], Q[5], Q[6]]
            bi = [0]

            def bank():
                b_ = B[bi[0] % 3]
                bi[0] += 1
                return b_

            def proj(slot, o, ps):
                for kc in range(8):
                    P.mm(ps, slot[:, kc, o:o + 128], xnt[:, kc, :], kc == 0, kc == 7)

            GA = T2(0)
            slot, idx = next_slab(("win", l), 2)
            ps = bank()
            proj(slot, 0, ps[:])
            yield
            P.copy("act", KT[l][:, 128:128 + TCH], ps[:])
            psv = bank()
            for t in range(NT):
                for kc in range(8):
                    P.mm(psv[:, t * 128:(t + 1) * 128], xnt[:, kc, t * 128:(t + 1) * 128],
                         slot[:, kc, 128:256], kc == 0, kc == 7)
            yield
            P.copy("dve", VA[l][:, 1:NT + 1, :, 0:64],
                   psv[:, :].rearrange("p (t g d) -> p t g d", t=NT, g=2))
            for cc in range(2):
                psa = bank()
                proj(slot, 256 + cc * 128, psa[:])
                yield
                P.copy("act", GA[:, cc, :], psa[:])
            release(idx)
            slot, idx = next_slab(("win", l), 3)
            for cc in range(2):
                psg = bank()
                proj(slot, cc * 128, psg[:])
                yield
                sgm = T2(1)[:, cc, :]
                sigm(sgm, psg[:])
                P.tt("dve", CIN[l][:, cc, 30:30 + TCH], GA[:, cc, :], sgm, ALU.mult)
            release(idx)
            YCc = T2(0)
            SQc = T2(1)

            def conv_chain():
                for cc in range(2):
                    P.ts("dve", YCc[:, cc, :], CIN[l][:, cc, 0:TCH], par(l, "ccw", cc * 31), ALU.mult,
                         par(l, "ccb", cc), ALU.add)
                yield None
                for k in range(1, 31):
                    for cc in range(2):
                        acc = YCc[:, cc, :]
                        P.stt(acc, CIN[l][:, cc, k:k + TCH], par(l, "ccw", cc * 31 + k), acc, ALU.mult, ALU.add)
                    yield None
                for cc in range(2):
                    P.act(SQc[:, cc, :], YCc[:, cc, :], AF.Square)
                yield None

            cchain = conv_chain()

            def cstep():
                try:
                    next(cchain)
                except StopIteration:
                    pass

            slot, idx = next_slab(("win", l), 1)
            for cq in range(4):
                ps = bank()
                proj(slot, cq * 128, ps[:])
                cstep()
                yield
                P.copy("dve" if cq % 2 else "act", QT[:, cq, :], ps[:])
            release(idx)
            slot, idx = next_slab(("win", l), 0)
            for cc in range(2):
                ps = bank()
                proj(slot, cc * 128, ps[:])
                cstep()
                yield
                P.copy("act", LX[l][:, cc, 3:3 + TCH], ps[:])
            for cc in range(2):
                ps = bank()
                proj(slot, 256 + cc * 128, ps[:])
                cstep()
                yield
                P.copy("act", GATE[:, cc, :], ps[:])
            release(idx)
            cstep()
            yield

            YBS = [MF[:, 6, :], MF[:, 17, :]]
            NU = 2 * NT

            def scores(u):
                b, g = u // 2, u % 2
                hasprev = (c * NT + b) > 0
                prt = slice(g * 64, (g + 1) * 64)
                sc, sp = Q[4], Q[5]
                qv = QT[prt, :, b * 128:(b + 1) * 128]
                P.mm(sc[:, :].rearrange("p (h q) -> p h q", h=4),
                     KT[l][prt, 128 + b * 128:128 + (b + 1) * 128], qv, True, False)
                P.mm(sc[:, :], IDB[:], MSK[:, 0, :], False, True)
                if hasprev:
                    P.mm(sp[:, :].rearrange("p (h q) -> p h q", h=4),
                         KT[l][prt, b * 128:(b + 1) * 128], qv, True, False)
                    P.mm(sp[:, :], IDB[:], MSK[:, 1, :], False, True)

            def expu(u):
                b, g = u // 2, u % 2
                hasprev = (c * NT + b) > 0
                P.act(PTB[:, g, 0, :], Q[4][:], AF.Exp, scale=0.125)
                if hasprev:
                    P.act(PTB[:, g, 1, :], Q[5][:], AF.Exp, scale=0.125)

            def pvu(u):
                b, g = u // 2, u % 2
                hasprev = (c * NT + b) > 0
                ptc, ptp = PTB[:, g, 0, :], PTB[:, g, 1, :]
                po = Q[6]
                for hh in range(4):
                    o = po[:, hh * 65:(hh + 1) * 65]
                    if hasprev:
                        P.mm(o, ptp[:, hh * 128:(hh + 1) * 128], VA[l][:, b, g, :], True, False)
                    P.mm(o, ptc[:, hh * 128:(hh + 1) * 128], VA[l][:, b + 1, g, :], not hasprev, True)

            def normu(u):
                b, g = u // 2, u % 2
                yb = YBS[b % 2]
                pov = Q[6][:, 0:260].rearrange("p (h e) -> p h e", h=4)
                den = small(4)
                P.tt("dve", den, pov[:, :, 64], par(l, "esnk", g * 4, 4), ALU.add)
                rden = small(4)
                P.recip(rden, den)
                rb = bass.AP(rden.tensor, rden.offset, [list(rden.ap[0]), [1, 4], [0, 64]])
                P.tt("dve", yb[:, g * 256:(g + 1) * 256].rearrange("p (h d) -> p h d", h=4),
                     pov[:, :, 0:64], rb, ALU.mult)

            gn_rs = {}

            def gn_act(b):
                ss = small()
                P.act(JUNK[:, 0:512], YBS[b % 2], AF.Square, accum=ss)
                gn_rs[b] = rstd_of(ss, 1.0 / 512, EPS)

            def gn_scale(b):
                P.ts("dve", YBN, YBS[b % 2], gn_rs[b], ALU.mult)

            def tr_cp(b):
                for cq in range(4):
                    P.tr(PT[:, cq * 128:(cq + 1) * 128], YBN[:, cq * 128:(cq + 1) * 128], IDB[:])
                P.copy("dve", YT[:, 2:6, b * 128:(b + 1) * 128],
                       PT[:, 0:512].rearrange("p (k c) -> p k c", k=4))

            for s_ in range(NU + 4):
                if s_ < NU:
                    scores(s_)
                if 1 <= s_ <= NU:
                    pvu(s_ - 1)
                if s_ >= 3 and (s_ - 3) % 2 == 0 and (s_ - 3) // 2 < NT:
                    gn_act((s_ - 3) // 2)
                cstep()
                yield
                if s_ < NU:
                    expu(s_)
                if 1 <= s_ <= NU:
                    normu(s_ - 1)
                if s_ >= 3 and (s_ - 3) % 2 == 0 and (s_ - 3) // 2 < NT:
                    gn_scale((s_ - 3) // 2)
                if s_ >= 4 and (s_ - 4) % 2 == 0 and (s_ - 4) // 2 < NT:
                    tr_cp((s_ - 4) // 2)
                cstep()
                yield

            for _ in cchain:
                yield

            def confpost():
                pb = Q[4]
                mean = MF[:, 6, :]
                m2 = T(4)
                var = T(5)
                P.mm(pb[:], ONES[:], YCc[:, 0, :], True, False)
                P.mm(pb[:], ONES[:], YCc[:, 1, :], False, True)
                yield
                P.act(mean, pb[:], AF.Identity)
                P.act(m2, pb[:], AF.Square)
                yield
                P.mm(pb[:], ONES[:], SQc[:, 0, :], True, False)
                P.mm(pb[:], ONES[:], SQc[:, 1, :], False, True)
                yield
                P.tt("dve", var, pb[:], m2, ALU.subtract)
                yield
                rsqrt_big(var, var, LN_EPS)
                yield
                d = YCc
                for cc in range(2):
                    P.tt("dve", d[:, cc, :], YCc[:, cc, :], mean, ALU.subtract)
                for cc in range(2):
                    P.tt("dve", d[:, cc, :], d[:, cc, :], var, ALU.mult)
                for cc in range(2):
                    P.ts("dve", d[:, cc, :], d[:, cc, :], par(l, "lng", cc), ALU.mult, par(l, "lnb", cc), ALU.add)
                yield
                for cc in range(2):
                    P.act(SQc[:, cc, :], d[:, cc, :], AF.Exp, scale=-1.0)
                for cc in range(2):
                    P.act(SQc[:, cc, :], SQc[:, cc, :], AF.Ln, bias=1.0)
                for cc in range(2):
                    P.act(SQc[:, cc, :], SQc[:, cc, :], AF.Exp, scale=-1.0)
                yield
                for cc in range(2):
                    P.tt("dve", d[:, cc, :], d[:, cc, :], SQc[:, cc, :], ALU.mult)
                yield
                P.act(SQc, d, AF.Square)
                yield
                P.mm(pb[:], ONES[:], SQc[:, 0, :], True, False)
                P.mm(pb[:], ONES[:], SQc[:, 1, :], False, True)
                yield
                sd = m2
                rsqrt_big(sd, pb[:], EPS)
                yield
                for cc in range(2):
                    P.tt("dve", YT[:, 6 + cc, :], d[:, cc, :], sd, ALU.mult)
                yield

            def lru():
                cps = [Q[5], Q[6]]
                for cc in range(2):
                    for k in range(4):
                        P.mm(cps[cc][:], DG[:, cc * 4 + k, :], LX[l][:, cc, k:k + TCH], k == 0, k == 3)
                yield
                xc = MF[:, 13:15, :]
                for cc in range(2):
                    P.act(xc[:, cc, :], cps[cc][:], AF.Identity, bias=par(l, "lcb", cc))
                yield
                P.copy("dve", XCB[:, :, :], xc)
                yield
                ig, r, a, h = MF[:, 15:17, :], MF[:, 18:20, :], MF[:, 20:22, :], MF[:, 22:24, :]
                for cc in range(2):
                    pr_, pi_ = Q[5], Q[6]
                    P.mm(pr_[:], WA[l][:, cc, :], XCB[:, cc, :], True, True)
                    P.mm(pi_[:], WX[l][:, cc, :], XCB[:, cc, :], True, True)
                    yield
                    sigm(r[:, cc, :], pr_[:], -1.0, par(l, "nba", cc))
                    sigm(ig[:, cc, :], pi_[:], -1.0, par(l, "nbx", cc))
                    yield
                for cc in range(2):
                    P.act(a[:, cc, :], r[:, cc, :], AF.Exp, scale=par(l, "c1", cc))
                    P.act(r[:, cc, :], r[:, cc, :], AF.Exp, scale=par(l, "c2", cc))
                s2 = r
                P.tt("dve", ig, ig, xc, ALU.mult)
                P.act(s2, s2, AF.Ln, scale=-1.0, bias=1.0)
                P.act(s2, s2, AF.Exp, scale=0.5)
                yield
                P.tt("dve", ig, ig, s2, ALU.mult)
                yield
                for cc in range(2):
                    h0 = HST[l][:, cc:cc + 1]
                    P.add("dve", lambda e, h_=h[:, cc, :], a_=a[:, cc, :], u_=ig[:, cc, :], h0=h0:
                          e.tensor_tensor_scan(h_, a_, u_, h0, ALU.mult, ALU.add),
                          [a[:, cc, :], ig[:, cc, :], h0], [h[:, cc, :]])
                    P.copy("pool", h0, h[:, cc, TCH - 1:TCH])
                w = r
                P.act(w, GATE, AF.Square)
                yield
                P.ts("dve", w, w, 0.044715, ALU.mult, 1.0, ALU.add)
                P.tt("dve", w, w, GATE, ALU.mult)
                yield
                sgg = a
                sigm(sgg, w, -1.5957691216057308)
                yield
                P.tt("dve", sgg, sgg, GATE, ALU.mult)
                P.tt("dve", YA, sgg, h, ALU.mult)
                yield
                P.act(SQ, YA, AF.Square)
                yield
                pn = Q[5]
                P.mm(pn[:], ONES[:], SQ[:, 0, :], True, False)
                P.mm(pn[:], ONES[:], SQ[:, 1, :], False, True)
                yield
                sd2 = MF[:, 13, :]
                rsqrt_big(sd2, pn[:], EPS)
                yield
                for cc in range(2):
                    P.tt("dve", YT[:, cc, :], YA[:, cc, :], sd2, ALU.mult)
                yield

            subs = [lru(), confpost()]
            while subs:
                for g_ in list(subs):
                    try:
                        next(g_)
                    except StopIteration:
                        subs.remove(g_)
                yield

            so0, io0 = next_slab(("wout", l), 0)
            so1, io1 = next_slab(("wout", l), 1)
            ob = [(Q[4], Q[5]), (Q[6], Q[4]), (Q[5], Q[6]), (Q[4], Q[5])]
            for t in range(NT):
                b0, b1 = ob[t]
                for dc in range(8):
                    P.mm(b0[:], YT[:, dc, t * 128:(t + 1) * 128], so0[:, dc, :], dc == 0, dc == 7)
                for dc in range(8):
                    P.mm(b1[:], YT[:, dc, t * 128:(t + 1) * 128], so1[:, dc, :], dc == 0, dc == 7)
                yield
                if has_next and t >= 1:
                    prenorm_tile(xr, xnt, t - 1)
                epilogue2(xr, t, b0[:], b1[:], gp, 1.0)
                yield
            release(io0)
            release(io1)
            if has_next:
                prenorm_tile(xr, xnt, NT - 1)
                yield
            P.copy("pool", KT[l][:, 0:128], KT[l][:, TCH:TCH + 128])
            P.copy("pool", VA[l][:, 0, :, :], VA[l][:, NT, :, :])
            P.copy("pool", LX[l][:, :, 0:3], LX[l][:, :, TCH:TCH + 3])
            P.copy("pool", CIN[l][:, :, 0:30], CIN[l][:, :, TCH:TCH + 30])

        step_counts = {}

        def run_slot(items):
            st_ = [[name, kind, gen, 0] for name, kind, gen in items]
            while st_:
                st_.sort(key=lambda it: (it[3] + 1) / float(step_counts.get(it[0], 1 << 30)))
                it = st_[0]
                cur_k[0] = it[1]
                try:
                    next(it[2])
                    it[3] += 1
                except StopIteration:
                    if P.dry:
                        step_counts[it[0]] = max(step_counts.get(it[0], 0), it[3])
                    st_.remove(it)

        seqs = [(l, st) for l in range(depth) for st in stages]
        nst = len(seqs)

        def make(c, k, overl):
            l, st = seqs[k]
            do_pre = k == 0
            has_next = k + 1 < nst
            if st == "mix":
                return ("mix", 1, mixer_gen(c, l, do_pre, has_next))
            nm = "ffx" if overl else "ffn"
            return (nm, 0, ffn_gen(c, l, 1 if st == "f1" else 2, do_pre, has_next, overl))

        two_stream = (depth == 2 and tuple(stages) == ("f1", "mix", "f2"))

        def record_main():
            ring_reset()
            if two_stream:
                load_x(0)
                if nch > 1:
                    load_x(1)
                for c in range(nch + 1):
                    items = []
                    if c >= 1:
                        build_diag(1)
                        items.append(make(c - 1, 4, False))
                    if c < nch:
                        items.append(make(c, 0, c >= 1))
                    run_slot(items)
                    items = []
                    if c < nch:
                        build_diag(0)
                        items.append(make(c, 1, False))
                    if c >= 1:
                        items.append(make(c - 1, 5, c < nch))
                    run_slot(items)
                    if c >= 1:
                        store_x(c - 1)
                        if c + 1 < nch:
                            load_x(c + 1)
                    if c < nch:
                        run_slot([make(c, 2, False)])
                        run_slot([make(c, 3, False)])
            else:
                load_x(0)
                for c in range(nch):
                    if c + 1 < nch:
                        load_x(c + 1)
                    for k, (l, st) in enumerate(seqs):
                        if st == "mix":
                            build_diag(l)
                        run_slot([make(c, k, False)])
                    store_x(c)

        P.dry = True
        record_main()
        ring["plan"] = []
        frozen = dict(step_counts)
        record_main()
        assert frozen == step_counts
        P.dry = False
        record_main()
        P.add("pool", lambda e: e.memset(SM[:, 0:1], 0.0), [out_d[:, :]], [SM[:, 0:1]])
        P.emit_all(es)
    return nc


_CACHE = {}


def kernel(**inputs):
    if "nc" not in _CACHE:
        _CACHE["nc"] = build()
    nc = _CACHE["nc"]
    x = np.ascontiguousarray(inputs["x"], dtype=np.float32)
    ws = {n: np.ascontiguousarray(inputs[n], dtype=np.float32) for n in WNAMES}
    in_maps = []
    for b in range(8):
        m = {"x": x[b]}
        m.update(ws)
        in_maps.append(m)
    res = run_bass_kernel_spmd(nc, in_maps, core_ids=list(range(8)))
    return np.stack([r["out"] for r in res.results], axis=0)
```
